# Optimizing a Trainium2 kernel written in Bass

```python
import math
import jax, jax.numpy as jnp
from jax import lax
import numpy as np

D_MODEL = 1024
BATCH = 8
SEQ = 4096
DEPTH = 4

HEAD_DIM = 64
SB_HEADS = 8
SWA_HEADS = 8
SWA_KV_HEADS = 2
WINDOW = 128
BLOCK = 128
D_FF = 4 * D_MODEL
ROPE_THETA = 10000.0
NORM_EPS = 1e-6
N_BRANCHES = 2

SB_WIDTH = SB_HEADS * HEAD_DIM
SWA_Q_WIDTH = SWA_HEADS * HEAD_DIM
SWA_KV_WIDTH = SWA_KV_HEADS * HEAD_DIM
IN_WIDTH = 3 * SB_WIDTH + SWA_Q_WIDTH + 2 * SWA_KV_WIDTH + N_BRANCHES * D_MODEL

kernel_name = "hybrid_stickbreak_swa_sink_gated_trunk"


def rms_norm(x, g):
    xf = x.astype(jnp.float32)
    y = xf * lax.rsqrt(jnp.mean(xf * xf, axis=-1, keepdims=True) + NORM_EPS)
    return (y * g.astype(jnp.float32)).astype(x.dtype)


def rope_tables(seq):
    inv_freq = 1.0 / (ROPE_THETA ** (jnp.arange(0, HEAD_DIM, 2, dtype=jnp.float32) / HEAD_DIM))
    ang = jnp.arange(seq, dtype=jnp.float32)[:, None] * inv_freq[None, :]
    return jnp.cos(ang), jnp.sin(ang)


def apply_rope(x, cos, sin):
    c = cos[None, :, None, :].astype(x.dtype)
    s = sin[None, :, None, :].astype(x.dtype)
    x1, x2 = jnp.split(x, 2, axis=-1)
    return jnp.concatenate([x1 * c - x2 * s, x2 * c + x1 * s], axis=-1)


def stick_breaking_attention(q, k, v):
    B, S, H, d = q.shape
    scale = d ** -0.5
    outs = []
    for blk in range(S // BLOCK):
        t0, t1 = blk * BLOCK, (blk + 1) * BLOCK
        qb = q[:, t0:t1]
        kb = k[:, :t1]
        vb = v[:, :t1]
        z = jnp.einsum('bqhd,bshd->bhqs', qb, kb).astype(jnp.float32) * scale
        t_idx = jnp.arange(t0, t1)[:, None]
        s_idx = jnp.arange(t1)[None, :]
        strict = s_idx < t_idx
        log_keep = jnp.where(strict, -jax.nn.softplus(z), 0.0)
        tail = lax.cumsum(log_keep, axis=3, reverse=True) - log_keep
        w = jnp.where(strict, jnp.exp(jax.nn.log_sigmoid(z) + tail), 0.0)
        outs.append(jnp.einsum('bhqs,bshd->bqhd', w.astype(v.dtype), vb))
    return jnp.concatenate(outs, axis=1)


def sliding_window_sink_attention(q, k, v, sinks):
    B, S, Hq, d = q.shape
    G = Hq // SWA_KV_HEADS
    nb = S // BLOCK
    scale = d ** -0.5
    qb = q.reshape(B, nb, BLOCK, SWA_KV_HEADS, G, d)
    pad = ((0, 0), (BLOCK, 0), (0, 0), (0, 0))
    kb = jnp.pad(k, pad).reshape(B, nb + 1, BLOCK, SWA_KV_HEADS, d)
    vb = jnp.pad(v, pad).reshape(B, nb + 1, BLOCK, SWA_KV_HEADS, d)
    k_band = jnp.concatenate([kb[:, :-1], kb[:, 1:]], axis=2)
    v_band = jnp.concatenate([vb[:, :-1], vb[:, 1:]], axis=2)
    scores = jnp.einsum('bnqkgd,bnskd->bnkgqs', qb, k_band).astype(jnp.float32) * scale
    i = jnp.arange(BLOCK)[:, None]
    j = jnp.arange(2 * BLOCK)[None, :]
    rel = j - BLOCK - i
    in_window = (rel <= 0) & (rel > -WINDOW)
    key_pos = jnp.arange(nb)[:, None, None] * BLOCK + j[None] - BLOCK
    valid = in_window[None] & (key_pos >= 0)
    scores = jnp.where(valid[None, :, None, None], scores, -jnp.inf)
    sink = jnp.broadcast_to(sinks.astype(jnp.float32).reshape(1, 1, SWA_KV_HEADS, G, 1, 1),
                            scores.shape[:-1] + (1,))
    probs = jax.nn.softmax(jnp.concatenate([scores, sink], axis=-1), axis=-1)[..., :-1]
    out = jnp.einsum('bnkgqs,bnskd->bnqkgd', probs.astype(v.dtype), v_band)
    return out.reshape(B, S, Hq * d)


def setup_inputs(seed: int = 0) -> dict:
    key = jax.random.key(seed)
    ks = jax.random.split(key, 12)
    nrm = lambda k, shape, scale: jax.random.normal(k, shape, jnp.float32) * scale
    return {
        "x": nrm(ks[0], (BATCH, SEQ, D_MODEL), 1.0),
        "mix_norm_g": 1.0 + nrm(ks[1], (DEPTH, D_MODEL), 0.02),
        "w_in": nrm(ks[2], (DEPTH, D_MODEL, IN_WIDTH), D_MODEL ** -0.5),
        "q_norm_g": 1.0 + nrm(ks[3], (DEPTH, HEAD_DIM), 0.02),
        "k_norm_g": 1.0 + nrm(ks[4], (DEPTH, HEAD_DIM), 0.02),
        "sinks": nrm(ks[5], (DEPTH, SWA_HEADS), 0.5),
        "w_branch_sb": nrm(ks[6], (DEPTH, SB_WIDTH, D_MODEL), SB_WIDTH ** -0.5),
        "w_branch_swa": nrm(ks[7], (DEPTH, SWA_Q_WIDTH, D_MODEL), SWA_Q_WIDTH ** -0.5),
        "w_out": nrm(ks[8], (DEPTH, D_MODEL, D_MODEL), D_MODEL ** -0.5),
        "mlp_norm_g": 1.0 + nrm(ks[9], (DEPTH, D_MODEL), 0.02),
        "w_up": nrm(ks[10], (DEPTH, D_MODEL, D_FF), D_MODEL ** -0.5),
        "w_down": nrm(ks[11], (DEPTH, D_FF, D_MODEL), D_FF ** -0.5),
    }


def reference(x, mix_norm_g, w_in, q_norm_g, k_norm_g, sinks, w_branch_sb, w_branch_swa,
              w_out, mlp_norm_g, w_up, w_down):
    B, S, D = x.shape
    cos, sin = rope_tables(S)
    split_at = np.cumsum([SB_WIDTH, SB_WIDTH, SB_WIDTH, SWA_Q_WIDTH, SWA_KV_WIDTH, SWA_KV_WIDTH]).tolist()
    for l in range(DEPTH):
        h = rms_norm(x, mix_norm_g[l])
        proj = h @ w_in[l]
        sb_q, sb_k, sb_v, sw_q, sw_k, sw_v, gate_logits = jnp.split(proj, split_at, axis=-1)

        to_heads = lambda t, n: t.reshape(B, S, n, HEAD_DIM)
        o_sb = stick_breaking_attention(to_heads(sb_q, SB_HEADS), to_heads(sb_k, SB_HEADS),
                                        to_heads(sb_v, SB_HEADS)).reshape(B, S, SB_WIDTH)
        y_sb = o_sb @ w_branch_sb[l]

        q = apply_rope(rms_norm(to_heads(sw_q, SWA_HEADS), q_norm_g[l]), cos, sin)
        k = apply_rope(rms_norm(to_heads(sw_k, SWA_KV_HEADS), k_norm_g[l]), cos, sin)
        v = to_heads(sw_v, SWA_KV_HEADS)
        y_swa = sliding_window_sink_attention(q, k, v, sinks[l]) @ w_branch_swa[l]

        gates = jax.nn.sigmoid(gate_logits.astype(jnp.float32)).astype(x.dtype).reshape(B, S, N_BRANCHES, D)
        merged = gates[:, :, 0] * y_sb + gates[:, :, 1] * y_swa
        x = x + merged @ w_out[l]

        h2 = rms_norm(x, mlp_norm_g[l])
        x = x + jnp.square(jax.nn.relu(h2 @ w_up[l])) @ w_down[l]
    return x
```

```python
import os
import numpy as np
import concourse.bass as bass
import concourse.mybir as mybir
from concourse.bass_utils import run_bass_kernel_spmd
from contextlib import ExitStack

F32, BF16 = mybir.dt.float32, mybir.dt.bfloat16
AF = mybir.ActivationFunctionType
ALU = mybir.AluOpType

D = 1024
KC = 8
T = 512
DFF = 4096
SLAB = 2048
NSLOT = 5
NSLAB = 58
NGV = 22
NCB = 1280
NCST = NCB + 128
EPS = 1e-6
LVL = float(os.environ.get('K_LVL', '99'))
NFILL = int(os.environ.get('K_NFILL', '2'))
NFILL2 = int(os.environ.get('K_NFILL2', '1'))
NEG = -30000.0

ENGS = ["pe", "act", "dve", "pool", "sp"]


class Sched:
    def __init__(self):
        self.ops = []
        self.q = {e: [] for e in ENGS}
        self.last_w = {}
        self.readers = {}
        self.dma_last = {}
        self.tag = ''

    def add(self, eng, fn, reads=(), writes=(), dma=None, extra_deps=()):
        op = dict(id=len(self.ops), eng=eng, fn=fn, deps=set(extra_deps), dma=dma, sig=False, val=None, tag=self.tag)
        preads = [r for r in reads if isinstance(r, tuple) and r[0] == "P"]
        if preads:
            reads = [r for r in reads if r not in preads]
            writes = list(writes) + preads
        for r in reads:
            if r in self.last_w:
                op["deps"].add(self.last_w[r])
        for w in writes:
            if w in self.last_w:
                op["deps"].add(self.last_w[w])
            for t in self.readers.get(w, {}).values():
                op["deps"].add(t)
        if dma is not None and dma in self.dma_last:
            op["deps"].add(self.dma_last[dma])
        key = eng if dma is None else ("dma", dma)
        for r in reads:
            self.readers.setdefault(r, {})[key] = op["id"]
        for w in writes:
            self.last_w[w] = op["id"]
            self.readers[w] = {}
        if dma is not None:
            self.dma_last[dma] = op["id"]
        op["deps"].discard(op["id"])
        self.ops.append(op)
        self.q[eng].append(op)
        return op["id"]

    def needs_wait(self, a, b):
        if a["dma"] is not None:
            return True
        if a["eng"] == b["eng"]:
            return a["eng"] != "pe"
        return True

    def finalize(self):
        for b in self.ops:
            for ai in b["deps"]:
                a = self.ops[ai]
                if self.needs_wait(a, b):
                    a["sig"] = True
        cnt = {e: 0 for e in ENGS}
        dcnt = {}
        for op in self.ops:
            if op["dma"] is not None:
                dcnt[op["dma"]] = dcnt.get(op["dma"], 0) + 16
                op["val"] = dcnt[op["dma"]]
            elif op["sig"]:
                cnt[op["eng"]] += 1
                op["val"] = cnt[op["eng"]]

    def replay(self, eng, e, esem, dsem):
        waited = {}
        for op in self.q[eng]:
            need = {}
            for ai in op["deps"]:
                a = self.ops[ai]
                if not self.needs_wait(a, op):
                    continue
                s = ("d", a["dma"]) if a["dma"] is not None else ("e", a["eng"])
                need[s] = max(need.get(s, 0), a["val"])
            for s, v in need.items():
                if waited.get(s, 0) >= v:
                    continue
                waited[s] = v
                h = dsem[s[1]] if s[0] == "d" else esem[s[1]]
                e.wait_ge(h, v)
            if op["fn"] is None:
                continue
            ins = op["fn"](e)
            if op["dma"] is not None:
                ins.then_inc(dsem[op["dma"]], 16)
            elif op["sig"]:
                ins.then_inc(esem[eng], 1)


def build(S, depth):
    NT = S // T
    NB = S // 128
    nc = bass.Bass("TRN2", target_bir_lowering=False)
    xT_d = nc.dram_tensor("xT", [KC, 128, S], F32, kind="ExternalInput").ap()
    wsl_d = nc.dram_tensor("wsl", [depth * NSLAB, 128, SLAB], F32, kind="ExternalInput").ap()
    gv_d = nc.dram_tensor("gv", [128, depth * NGV], F32, kind="ExternalInput").ap()
    cst_d = nc.dram_tensor("cst", [128, NCST], F32, kind="ExternalInput").ap()
    rope_d = nc.dram_tensor("rope", [2, 128, S], F32, kind="ExternalInput").ap()
    yT_d = nc.dram_tensor("yT", [KC, 128, S], F32, kind="ExternalOutput").ap()
    wbf_d = nc.dram_tensor("wbf", [depth * NSLAB, 128, SLAB], BF16).ap()
    xres_d = nc.dram_tensor("xres", [KC, 128, S], F32).ap()

    sb_specs = [
        ("ksb", [128, 4, S], BF16), ("vsb", [128, NB, 512], BF16),
        ("kw", [128, 2, 640], BF16), ("vw", [128, 5, 128], BF16),
        ("xt2", [128, 2 * KC, T], F32), ("hT", [128, KC, T], BF16), ("h2T", [128, KC, T], BF16),
        ("qm", [128, KC, T], BF16),
        ("osb", [128, 4, T], BF16), ("osw", [128, 4, T], BF16),
        ("uT", [128, 16, T], BF16),
        ("ring", [128, NSLOT, SLAB], BF16),
        ("ebuf", [128, 2, T], F32), ("spb", [128, 4, T], BF16), ("wb", [128, 4, T], BF16),
        ("spsum", [128, 8 * T], BF16),
        ("rstd", [128, T], F32),
        ("scr", [128, 6, T], F32),
        ("cb", [128, NCB], BF16), ("rt", [128, 128], F32),
        ("gv", [128, depth * NGV], F32), ("gq8", [128, depth], F32), ("esink", [128, depth * 4], F32),
    ]
    S_ = Sched()
    add = S_.add

    with ExitStack() as es:
        sb = {}
        for name, shape, dt in sb_specs:
            sb[name] = es.enter_context(nc.sbuf_tensor("s_" + name, shape, dt))
        PP = [es.enter_context(nc.psum_tensor(f"PP{i}", [128, 1024], F32)) for i in range(4)]

        class _Banks:
            def __getitem__(self, i):
                return PP[i // 2][:, (i % 2) * 512:(i % 2) * 512 + 512]

        P = _Banks()

        def PPv(k):
            return PP[k][:, :].rearrange("p (e c) -> p e c", e=2)

        esem = {e: es.enter_context(nc.semaphore(f"se_{e}")) for e in ENGS}
        dnames = [f"ring{i}" for i in range(NSLOT)] + ["xld0", "xld1", "xldf", "xst0", "xst1", "rope", "cst", "gv", "rt"] + [f"cast{i}" for i in range(4)]
        dsem = {n: es.enter_context(nc.semaphore(f"sd_{n}")) for n in dnames}

        ksb, vsb, kw, vw, hT = sb["ksb"], sb["vsb"], sb["kw"], sb["vw"], sb["hT"]
        xt2 = sb["xt2"]
        h2T = sb["h2T"]

        def sqv(sl):
            return sb["spb"][:, 2 * sl, :]

        def pTv(k):
            return sb["wb"][:, 2 * k:2 * k + 2, :].rearrange("p a c -> p (a c)")

        qm, osb, osw, uT, ring = sb["qm"], sb["osb"], sb["osw"], sb["uT"], sb["ring"]
        ebuf, spb, wb, spsum, scr = sb["ebuf"], sb["spb"], sb["wb"], sb["spsum"], sb["scr"]
        cb, gv = sb["cb"], sb["gv"]
        ident = cb[:, 0:128]
        negtri = cb[:, 128:256]
        negones = cb[:, 256:384]
        ones = cb[:, 384:512]
        bd = cb[:, 512:640]
        sbmask = cb[:, 640:768]
        RT = sb["rt"][:, :]

        def v3(ap, h):
            return ap.rearrange("p (h c) -> p h c", h=h)

        def sps(hh, pp, c0, n):
            o = (hh * 2 + pp) * T
            return spsum[:, o + c0:o + c0 + n]

        def swm(h0, c0, c1):
            return v3(cb[:, 768:1280], 2)[:, :, c0:c1]

        add("pool", lambda e: e.dma_start(out=cb[:, :], in_=cst_d[:, 0:NCB]), writes=["cb"], dma="cst")
        add("sp", lambda e: e.dma_start(out=sb["rt"][:, :], in_=cst_d[:, NCB:NCB + 128]), writes=["cstf"], dma="rt")
        add("sp", lambda e: e.dma_start(out=gv[:, :], in_=gv_d[:, :]), writes=["gv"], dma="gv")
        for l in range(depth):
            add("dve", lambda e, l=l: e.tensor_scalar(out=sb["gq8"][:, l:l + 1], in0=gv[:, l * NGV + 16:l * NGV + 17],
                                                      scalar1=0.125, scalar2=None, op0=ALU.mult),
                reads=["gv"], writes=[("gq8", l)])
            add("act", lambda e, l=l: e.activation(out=sb["esink"][:, 4 * l:4 * l + 4], in_=gv[:, l * NGV + 18:l * NGV + 22],
                                                   func=AF.Exp), reads=["gv"], writes=[("esink", l)])
        cst_state = dict(ci=0)

        def emit_cast(l, extra=()):
            bounds = [0, 2, 6, 14, 22, 30, 38, 46, 54, NSLAB]
            for gidx, (g0, g1) in enumerate(zip(bounds, bounds[1:])):
                a0, a1 = l * NSLAB + g0, l * NSLAB + g1
                add("pool", lambda e, a0=a0, a1=a1: e.dma_start(out=wbf_d[a0:a1], in_=wsl_d[a0:a1]),
                    writes=[("wbf", l, s) for s in range(g0, g1)], dma=f"cast{cst_state['ci'] % 4}",
                    extra_deps=(extra if gidx >= 1 else ()))
                cst_state["ci"] += 1

        cast_pending = []

        def trickle_cast():
            if cast_pending:
                l_, s_ = cast_pending.pop(0)
                a0 = l_ * NSLAB + s_
                add("pool", lambda e: e.dma_start(out=wbf_d[a0:a0 + 1], in_=wsl_d[a0:a0 + 1]),
                    writes=[("wbf", l_, s_)], dma=f"cast{cst_state['ci'] % 4}", extra_deps=[len(S_.ops) - 1])
                cst_state["ci"] += 1

        st = dict(next_dma=0, gbase=0, layer=0)

        def slab_region(g):
            return ("ring", g % NSLOT)

        NTL = depth * NT
        order = []
        for gi_ in range(NTL):
            l_ = gi_ // NT
            order += [(l_, n_) for n_ in range(0, 10)]
            if gi_ > 0:
                order += [((gi_ - 1) // NT, n_) for n_ in range(26, NSLAB)]
            order += [(l_, n_) for n_ in range(10, 26)]
        order += [(depth - 1, n_) for n_ in range(26, NSLAB)]
        st["pos"] = -1

        def begin_slab(n, layer=None, live=0):
            tgt = (st["layer"] if layer is None else layer, n)
            if st["pos"] < 0 or order[st["pos"]] != tgt:
                st["pos"] += 1
                assert order[st["pos"]] == tgt, (order[st["pos"]], tgt)
            g = st["pos"]
            upto = min(g + NSLOT - live, len(order))
            while st["next_dma"] < upto:
                gg = st["next_dma"]
                ll, nn = order[gg]
                slot = gg % NSLOT
                add("sp", lambda e, slot=slot, src=ll * NSLAB + nn: e.dma_start(out=ring[:, slot, :], in_=wbf_d[src]),
                    reads=[("wbf", ll, nn)], writes=[("ring", slot)], dma=f"ring{slot}")
                st["next_dma"] += 1
            return g % NSLOT

        def wfm(slot, off, kcn):
            return ring[:, slot, off:off + kcn * 128].rearrange("p (k m) -> p k m", k=kcn)

        pb = dict(i=0)

        def pbank(lst):
            pb["i"] += 1
            return lst[pb["i"] % len(lst)]

        MMF = {
            "n1": lambda o, l_, r, a_, b_, k: (lambda e: e.matmul(o, lhsT=l_, rhs=r, start=a_, stop=b_, **k)),
            "proj": lambda o, l_, r, a_, b_, k: (lambda e: e.matmul(o, lhsT=l_, rhs=r, start=a_, stop=b_, **k)),
            "swa": lambda o, l_, r, a_, b_, k: (lambda e: e.matmul(o, lhsT=l_, rhs=r, start=a_, stop=b_, **k)),
            "sb": lambda o, l_, r, a_, b_, k: (lambda e: e.matmul(o, lhsT=l_, rhs=r, start=a_, stop=b_, **k)),
            "merge": lambda o, l_, r, a_, b_, k: (lambda e: e.matmul(o, lhsT=l_, rhs=r, start=a_, stop=b_, **k)),
            "wout": lambda o, l_, r, a_, b_, k: (lambda e: e.matmul(o, lhsT=l_, rhs=r, start=a_, stop=b_, **k)),
            "mlp": lambda o, l_, r, a_, b_, k: (lambda e: e.matmul(o, lhsT=l_, rhs=r, start=a_, stop=b_, **k)),
        }

        def mm(out, lhsT, rhs, start, stop, reads, writes, **kw_):
            i = add("pe", MMF[S_.tag](out, lhsT, rhs, start, stop, kw_), reads=reads, writes=writes)
            n = 1
            for d in rhs.shape[1:]:
                n *= d
            S_.ops[i]["shape"] = "%d*%d*%d" % (lhsT.shape[0], lhsT.shape[1], n)
            return i

        def rmsnorm(l, gcol, xt, xp, dst, dreg):
            bank = 6
            for kc in range(KC):
                sl = kc % 2
                add("act", lambda e, kc=kc, sl=sl: e.activation(out=sqv(sl), in_=xt[:, kc, :], func=AF.Square),
                    reads=[("xt", xp, kc)], writes=[("spb", sl)])
                mm(P[bank][:, :], ones, sqv(sl), kc == 0, kc == KC - 1,
                   reads=["cb", ("spb", sl)], writes=[("P", bank)])
            add("act", lambda e: e.activation(out=sb["rstd"][:, :], in_=P[bank][:, :], func=AF.Ln, scale=1.0 / D, bias=EPS),
                reads=[("P", bank)], writes=["rstd"])
            add("act", lambda e: e.activation(out=sb["rstd"][:, :], in_=sb["rstd"][:, :], func=AF.Exp, scale=-0.5),
                reads=["rstd"], writes=["rstd"])
            for kc in range(KC):
                add("dve", lambda e, kc=kc: e.scalar_tensor_tensor(out=dst[:, kc, :], in0=xt[:, kc, :],
                                                                   scalar=gv[:, l * NGV + gcol + kc:l * NGV + gcol + kc + 1],
                                                                   in1=sb["rstd"][:, :], op0=ALU.mult, op1=ALU.mult),
                    reads=[("xt", xp, kc), "rstd", "gv"], writes=[(dreg, kc)])

        def norm_sq(xt, xp, base=0, eng="act"):
            for kc in range(KC):
                if eng == "act":
                    add("act", lambda e, kc=kc: e.activation(out=uT[:, base + kc, :], in_=xt[:, kc, :], func=AF.Square),
                        reads=[("xt", xp, kc)], writes=[("uT", base + kc)])
                else:
                    add("pool", lambda e, kc=kc: e.tensor_tensor(out=uT[:, base + kc, :], in0=xt[:, kc, :], in1=xt[:, kc, :], op=ALU.mult),
                        reads=[("xt", xp, kc)], writes=[("uT", base + kc)])

        def norm_rest(l, gcol, xt, xp, dst, dreg, base=0):
            bank = 6
            for kc in range(KC):
                mm(P[bank][:, :], ones, uT[:, base + kc, :], kc == 0, kc == KC - 1,
                   reads=["cb", ("uT", base + kc)], writes=[("P", bank)])
            add("act", lambda e: e.activation(out=sb["rstd"][:, :], in_=P[bank][:, :], func=AF.Ln, scale=1.0 / D, bias=EPS),
                reads=[("P", bank)], writes=["rstd"])
            add("act", lambda e: e.activation(out=sb["rstd"][:, :], in_=sb["rstd"][:, :], func=AF.Exp, scale=-0.5),
                reads=["rstd"], writes=["rstd"])
            for kc in range(KC):
                add("dve", lambda e, kc=kc: e.scalar_tensor_tensor(out=dst[:, kc, :], in0=xt[:, kc, :],
                                                                   scalar=gv[:, l * NGV + gcol + kc:l * NGV + gcol + kc + 1],
                                                                   in1=sb["rstd"][:, :], op0=ALU.mult, op1=ALU.mult),
                    reads=[("xt", xp, kc), "rstd", "gv"], writes=[(dreg, kc)])

        def proj_fm(slot, off, kcn, src, src_reg, bank):
            w = wfm(slot, off, kcn)
            for kc in range(kcn):
                mm(P[bank][:, :], w[:, kc, :], src[:, kc, :], kc == 0, kc == kcn - 1,
                   reads=[("ring", slot), (src_reg, kc)], writes=[("P", bank)])

        def emit_load(gi, eng="pool"):
            l_, ti_ = divmod(gi, NT)
            xp_ = gi % 2
            src = xT_d if l_ == 0 else xres_d
            t0_ = ti_ * T
            return add(eng, lambda e: e.dma_start(out=xt2[:, xp_ * KC:(xp_ + 1) * KC, :], in_=src.rearrange("k p s -> p k s")[:, :, t0_:t0_ + T]),
                       reads=[("xres", ti_)] if l_ > 0 else [], writes=[("xt", xp_, kc) for kc in range(KC)],
                       dma=f"xld{xp_}")

        def mlp_gen(gp):
            lp, tip = divmod(gp, NT)
            xpp = gp % 2
            xtp = xt2[:, xpp * KC:(xpp + 1) * KC, :]
            MB = [6, 7]
            for hh in range(2):
                for j in range(16):
                    slot = begin_slab(26 + 16 * hh + j // 2, layer=lp)
                    bank = pbank(MB)
                    w = wfm(slot, (j % 2) * 1024, KC)
                    rs = j % 2
                    for kc in range(KC):
                        mm(P[bank][:, :], w[:, kc, :], h2T[:, kc, :], kc == 0, kc == KC - 1,
                           reads=[("ring", slot), ("h2T", kc)], writes=[("P", bank)])
                        if kc == KC - 1:
                            add("dve", lambda e, bank=bank, rs=rs: e.tensor_scalar(out=scr[:, rs, :], in0=P[bank][:, :], scalar1=0.0, scalar2=None, op0=ALU.max),
                                reads=[("P", bank)], writes=[("scr", rs)])
                            add("dve", lambda e, rs=rs, j=j: e.tensor_tensor(out=uT[:, j, :], in0=scr[:, rs, :], in1=scr[:, rs, :], op=ALU.mult),
                                reads=[("scr", rs)], writes=[("uT", j)])
                        yield
                for c in range(KC):
                    slot = begin_slab(26 + 16 * hh + 8 + c, layer=lp)
                    bank = pbank(MB)
                    w = wfm(slot, 0, 16)
                    for kc in range(16):
                        mm(P[bank][:, :], w[:, kc, :], uT[:, kc, :], kc == 0, kc == 15,
                           reads=[("ring", slot), ("uT", kc)], writes=[("P", bank)])
                        if kc == 15:
                            add("dve", lambda e, c=c, bank=bank: e.tensor_tensor(out=xtp[:, c, :], in0=P[bank][:, :], in1=xtp[:, c, :], op=ALU.add),
                                reads=[("P", bank), ("xt", xpp, c)], writes=[("xt", xpp, c)])
                        yield
            if tip == NT - 1:
                while cast_pending:
                    trickle_cast()
            dst = yT_d if lp == depth - 1 else xres_d
            add("sp", lambda e: e.dma_start(out=dst.rearrange("k p s -> p k s")[:, :, tip * T:(tip + 1) * T], in_=xtp),
                reads=[("xt", xpp, kc) for kc in range(KC)], writes=[("xres", tip)] if lp < depth - 1 else [("yT", tip)], dma=f"xst{xpp}")

        ld0 = emit_load(0, eng="sp")
        emit_cast(0, extra=[ld0])
        for l in range(depth):
            for ti in range(NT):
                t0 = ti * T
                b0 = ti * 4
                gi = l * NT + ti
                xp = gi % 2
                xt = xt2[:, xp * KC:(xp + 1) * KC, :]
                st["layer"] = l
                last_layer = (l == depth - 1)
                S_.tag = 'load'
                if ti == 0 and l + 1 < depth:
                    cast_pending.extend((l + 1, s_) for s_ in range(NSLAB))
                add("sp", lambda e, t0=t0: e.dma_start(out=ebuf[:, :, :], in_=rope_d.rearrange("c p s -> p c s")[:, :, t0:t0 + T]),
                    writes=["ebuf"], dma="rope")
                S_.tag = 'n1'
                if gi == 0:
                    rmsnorm(l, 0, xt, xp, hT, "hT")
                S_.tag = 'proj'
                if LVL >= 2:
                    PB = [0, 1, 2, 3, 4, 5, 6, 7]
                    for c in range(4):
                        slot = begin_slab(c // 2)
                        bank = pbank(PB)
                        proj_fm(slot, (c % 2) * 1024, KC, hT, "hT", bank)
                        add("dve", lambda e, c=c, bank=bank: e.tensor_scalar(out=qm[:, c, :], in0=P[bank][:, :], scalar1=0.125, scalar2=None, op0=ALU.mult),
                            reads=[("P", bank)], writes=[("qm", c)])
                    for c in range(4 if LVL >= 2.2 else 0):
                        slot = begin_slab(2 + c // 2)
                        bank = pbank(PB)
                        proj_fm(slot, (c % 2) * 1024, KC, hT, "hT", bank)
                        add("dve", lambda e, c=c, bank=bank, t0=t0: e.tensor_copy(out=ksb[:, c, t0:t0 + T], in_=P[bank][:, :]),
                            reads=[("P", bank)], writes=[("ksb", c, ti)])
                    chb = {}

                    def chA(c):
                        slot = begin_slab(4 + c // 2)
                        bank = pbank(PB)
                        proj_fm(slot, (c % 2) * 1024, KC, hT, "hT", bank)
                        rs = c % 2
                        add("dve", lambda e: e.tensor_copy(out=scr[:, rs, :], in_=P[bank][:, :]),
                            reads=[("P", bank)], writes=[("scr", rs)])
                        add("act", lambda e: e.activation(out=sqv(rs), in_=P[bank][:, :], func=AF.Square),
                            reads=[("P", bank)], writes=[("spb", rs)])

                    def chB(c):
                        rs = c % 2
                        bank2 = pbank(PB)
                        mm(P[bank2][:, :], bd, sqv(rs), True, True, reads=["cb", ("spb", rs)], writes=[("P", bank2)])
                        add("act", lambda e: e.activation(out=sb["rstd"][:, :], in_=P[bank2][:, :], func=AF.Ln, scale=1.0 / 64, bias=EPS),
                            reads=[("P", bank2)], writes=["rstd"])
                        add("act", lambda e: e.activation(out=sb["rstd"][:, :], in_=sb["rstd"][:, :], func=AF.Exp, scale=-0.5),
                            reads=["rstd"], writes=["rstd"])
                        gcolap = sb["gq8"][:, l:l + 1] if c < 4 else gv[:, l * NGV + 17:l * NGV + 18]
                        add("dve", lambda e: e.scalar_tensor_tensor(out=scr[:, 2 + rs, :], in0=scr[:, rs, :], scalar=gcolap,
                                                                    in1=sb["rstd"][:, :], op0=ALU.mult, op1=ALU.mult),
                            reads=[("scr", rs), "rstd", "gv", ("gq8", l)], writes=[("scr", 2 + rs)])

                    def chC(c):
                        rs = c % 2
                        bank3 = pbank(PB)
                        mm(P[bank3][:, :], RT, scr[:, 2 + rs, :], True, True, reads=["cstf", ("scr", 2 + rs)], writes=[("P", bank3)])
                        add("dve", lambda e: e.tensor_tensor(out=scr[:, 4, :], in0=scr[:, 2 + rs, :], in1=ebuf[:, 0, :], op=ALU.mult),
                            reads=[("scr", 2 + rs), "ebuf"], writes=[("scr", 4)])
                        add("dve", lambda e: e.tensor_tensor(out=scr[:, 5, :], in0=P[bank3][:, :], in1=ebuf[:, 1, :], op=ALU.mult),
                            reads=[("P", bank3), "ebuf"], writes=[("scr", 5)])
                        if c < 4:
                            dst = qm[:, 4 + c, :]
                            dreg = ("qm", 4 + c)
                        else:
                            dst = kw[:, c - 4, 128:640]
                            dreg = ("kw", c - 4, 1)
                        add("dve", lambda e: e.tensor_tensor(out=dst, in0=scr[:, 4, :], in1=scr[:, 5, :], op=ALU.add),
                            reads=[("scr", 4), ("scr", 5)], writes=[dreg])

                    def vsb_tb(tb, s7, s8):
                        bank = pbank(PB)
                        for kc in range(KC):
                            slot = s7 if kc < 4 else s8
                            wv = ring[:, slot, :].rearrange("p (k n) -> p k n", k=4)[:, kc % 4, :]
                            mm(P[bank][:, :], hT[:, kc, tb * 128:(tb + 1) * 128], wv, kc == 0, kc == KC - 1,
                               reads=[("ring", slot), ("hT", kc)], writes=[("P", bank)])
                        blk = b0 + tb
                        add("dve", lambda e: e.tensor_copy(out=vsb[:, blk, :], in_=P[bank][:, :]),
                            reads=[("P", bank)], writes=[("vsb", blk)])

                    chA(0)
                    chA(1)
                    chB(0)
                    chA(2)
                    chB(1)
                    chC(0)
                    chA(3)
                    chB(2)
                    chC(1)
                    chA(4)
                    chB(3)
                    chC(2)
                    chA(5)
                    chB(4)
                    chC(3)
                    s7 = begin_slab(7)
                    s8 = begin_slab(8, live=1)
                    vsb_tb(0, s7, s8)
                    chB(5)
                    chC(4)
                    vsb_tb(1, s7, s8)
                    chC(5)
                    if gi > 0:
                        S_.tag = 'mlp'
                        gq_ = gi - 1
                        norm_sq(xt2[:, (gq_ % 2) * KC:(gq_ % 2 + 1) * KC, :], gq_ % 2, base=8, eng="pool")
                        S_.tag = 'proj'
                    vsb_tb(2, s7, s8)
                    vsb_tb(3, s7, s8)
                    s9 = begin_slab(9)
                    for tb in range(4):
                        bank = pbank(PB)
                        wv = ring[:, s9, 0:1024].rearrange("p (k n) -> p k n", k=8)
                        for kc in range(KC):
                            mm(P[bank][:, 0:128], hT[:, kc, tb * 128:(tb + 1) * 128], wv[:, kc, :], kc == 0, kc == KC - 1,
                               reads=[("ring", s9), ("hT", kc)], writes=[("P", bank)])
                        add("dve", lambda e, bank=bank, tb=tb: e.tensor_copy(out=vw[:, 1 + tb, :], in_=P[bank][:, 0:128]),
                            reads=[("P", bank)], writes=[("vw", 1 + tb)])

                S_.tag = 'swa'
                if LVL >= 3:
                    for g in range(2):
                        OB = [0, 1] if g == 0 else [6, 7]
                        DB = [2, 3]
                        SBK = [4, 5]
                        units = []
                        for kbr in range(-1 if ti > 0 else 0, 4):
                            for par in range(2):
                                units.append((kbr, par))
                        first = {(ci_, par): True for ci_ in range(2) for par in range(2)}

                        def ugeom(kbr):
                            qlo = max(kbr, 0)
                            qhi = min(kbr + 1, 3)
                            return qlo * 128, (qhi - qlo + 1) * 128, (0 if kbr >= 0 else 128)

                        def swS(i):
                            kbr, par = units[i]
                            c0, ncol, m0 = ugeom(kbr)
                            ps_ = slice(64 * par, 64 * par + 64)
                            sbk = SBK[i % 2]
                            kcols = slice((kbr + 1) * 128, (kbr + 2) * 128)
                            mm(v3(P[sbk][:, 0:2 * ncol], 2), kw[ps_, g, kcols], qm[ps_, 4 + 2 * g:4 + 2 * g + 2, c0:c0 + ncol], True, False,
                               reads=[("kw", g, 0 if kbr < 0 else 1)] + [("qm", 4 + 2 * g), ("qm", 4 + 2 * g + 1)], writes=[("P", sbk)])
                            mm(v3(P[sbk][:, 0:2 * ncol], 2), ident, swm(0, m0, m0 + ncol), False, True, reads=["cb"], writes=[("P", sbk)])

                        def swE(i):
                            kbr, par = units[i]
                            c0, ncol, m0 = ugeom(kbr)
                            sbk = SBK[i % 2]
                            add("act", lambda e: e.activation(out=pTv(i % 2)[:, 0:2 * ncol], in_=P[sbk][:, 0:2 * ncol], func=AF.Exp),
                                reads=[("P", sbk)], writes=[("wb", i % 2)])

                        def swPV(i):
                            kbr, par = units[i]
                            c0, ncol, m0 = ugeom(kbr)
                            ps_ = slice(64 * par, 64 * par + 64)
                            for ci_ in range(2):
                                rhs = pTv(i % 2)[:, ci_ * ncol:(ci_ + 1) * ncol]
                                fs = first[(ci_, par)]
                                first[(ci_, par)] = False
                                mm(P[OB[ci_]][ps_, c0:c0 + ncol], vw[:, kbr + 1, 64 * g:64 * g + 64], rhs, fs, False,
                                   reads=[("vw", kbr + 1), ("wb", i % 2)], writes=[("P", OB[ci_])], skip_group_check=True)
                                mm(P[DB[ci_]][ps_, c0:c0 + ncol], ones[:, 0:64], rhs, fs, False,
                                   reads=["cb", ("wb", i % 2)], writes=[("P", DB[ci_])], skip_group_check=True)

                        swS(0)
                        for i in range(len(units)):
                            if i + 1 < len(units):
                                swS(i + 1)
                            swE(i)
                            swPV(i)
                        for ci_ in range(2):
                            c = 2 * g + ci_
                            add("dve", lambda e, ci_=ci_, c=c, l=l, DB=DB: e.tensor_scalar(out=scr[:, 2 + ci_, :], in0=P[DB[ci_]][:, :], scalar1=sb["esink"][:, 4 * l + c:4 * l + c + 1],
                                                                              scalar2=None, op0=ALU.add),
                                reads=[("P", DB[ci_]), ("esink", l)], writes=[("scr", 2 + ci_)])
                        for ci_ in range(2):
                            c = 2 * g + ci_
                            add("dve", lambda e, ci_=ci_: e.reciprocal(out=scr[:, 2 + ci_, :], in_=scr[:, 2 + ci_, :]),
                                reads=[("scr", 2 + ci_)], writes=[("scr", 2 + ci_)])
                            add("dve", lambda e, ci_=ci_, c=c, OB=OB: e.tensor_tensor(out=osw[:, c, :], in0=P[OB[ci_]][:, :], in1=scr[:, 2 + ci_, :], op=ALU.mult),
                                reads=[("P", OB[ci_]), ("scr", 2 + ci_)], writes=[("osw", c)])
                    if ti < NT - 1:
                        for g in range(2):
                            add("dve", lambda e, g=g: e.tensor_copy(out=kw[:, g, 0:128], in_=kw[:, g, 512:640]),
                                reads=[("kw", g, 1)], writes=[("kw", g, 0)])
                        add("dve", lambda e: e.tensor_copy(out=vw[:, 0, :], in_=vw[:, 4, :]), reads=[("vw", 4)], writes=[("vw", 0)])

                filler = None
                if gi > 0:
                    S_.tag = 'mlp'
                    gpv = gi - 1
                    norm_rest(gpv // NT, 8, xt2[:, (gpv % 2) * KC:(gpv % 2 + 1) * KC, :], gpv % 2, h2T, "h2T", base=8)
                    filler = mlp_gen(gpv)

                def fill(k):
                    nonlocal_f = filler
                    if nonlocal_f is None:
                        return
                    S_.tag = 'mlp'
                    for _ in range(k):
                        try:
                            next(nonlocal_f)
                        except StopIteration:
                            break
                    S_.tag = 'sb'

                S_.tag = 'sb'
                if LVL >= 4:
                    for hg in range(2):
                        CB = [2, 3]
                        BPAIR = [0, 0]
                        APAIR = 2
                        add("dve", lambda e: e.memset(spsum[:, :], 0.0), writes=[("spsum", hh, pp) for hh in range(4) for pp in range(2)])
                        steps = []
                        for kb in range(b0 + 3, -1, -1):
                            for hp in range(2):
                                steps.append((kb, hp))
                        npair = len(steps)
                        firstC = {hh: True for hh in range(4)}

                        def geom(kb):
                            j = max(kb - b0, 0)
                            return j * 128, T - j * 128, kb >= b0

                        def ebsel(p):
                            if p % 2 == 0:
                                return ebuf, ["ebuf"]
                            return scr[:, 4:6, :], [("scr", 4), ("scr", 5)]

                        def spair(hp, pp, c0, n):
                            return spsum[:, 4 * hp * T:(4 * hp + 4) * T].rearrange("p (e q c) -> p e q c", e=2, q=2)[:, :, pp, c0:c0 + n]

                        def Zst(p):
                            kb, hp = steps[p]
                            ch = 2 * hg + hp
                            c0, n, diag = geom(kb)
                            for e_ in range(2):
                                ps_ = slice(64 * e_, 64 * e_ + 64)
                                a = 2 * APAIR + e_
                                mm(P[a][:, c0:c0 + n], ksb[ps_, ch, kb * 128:(kb + 1) * 128], qm[ps_, ch, c0:c0 + n], True, not diag,
                                   reads=[("ksb", ch, kb // 4), ("qm", ch)], writes=[("P", a)])
                            if diag:
                                for e_ in range(2):
                                    a = 2 * APAIR + e_
                                    mm(P[a][:, c0:c0 + 128], ident, sbmask, False, True, reads=["cb"], writes=[("P", a)])

                        def EAst(p):
                            kb, hp = steps[p]
                            c0, n, diag = geom(kb)
                            eb, ebr = ebsel(p)
                            add("act", lambda e: e.activation(out=eb[:, :, c0:c0 + n], in_=PPv(APAIR)[:, :, c0:c0 + n], func=AF.Exp),
                                reads=[("P", 2 * APAIR), ("P", 2 * APAIR + 1)], writes=ebr)

                        def LNst(p):
                            kb, hp = steps[p]
                            c0, n, diag = geom(kb)
                            o = 2 * (p % 2)
                            eb, ebr = ebsel(p)
                            add("act", lambda e: e.activation(out=spb[:, o:o + 2, c0:c0 + n], in_=eb[:, :, c0:c0 + n], func=AF.Ln, bias=1.0),
                                reads=ebr, writes=[("spb", p % 2)])

                        def Bst(p):
                            kb, hp = steps[p]
                            ch = 2 * hg + hp
                            c0, n, diag = geom(kb)
                            step = b0 + 3 - kb
                            old = step % 2
                            o = 2 * (p % 2)
                            bp = BPAIR[p % 2]
                            for e_ in range(2):
                                ps_ = slice(64 * e_, 64 * e_ + 64)
                                b = 2 * bp + e_
                                mm(P[b][:, c0:c0 + n], ksb[ps_, ch, kb * 128:(kb + 1) * 128], qm[ps_, ch, c0:c0 + n], True, False,
                                   reads=[("ksb", ch, kb // 4), ("qm", ch)], writes=[("P", b)])
                            for e_ in range(2):
                                b = 2 * bp + e_
                                hh = 2 * hp + e_
                                if diag:
                                    mm(P[b][:, c0:c0 + 128], ident, sbmask, False, False, reads=["cb"], writes=[("P", b)])
                                mm(P[b][:, c0:c0 + n], negtri, spb[:, o + e_, c0:c0 + n], False, step == 0,
                                   reads=["cb", ("spb", p % 2)], writes=[("P", b)])
                                if step > 0:
                                    mm(P[b][:, c0:c0 + n], negones, sps(hh, old, c0, n), False, True,
                                       reads=["cb", ("spsum", hh, old)], writes=[("P", b)])
                            add("dve", lambda e: e.tensor_tensor(out=spair(hp, 1 - old, c0, n), in0=spair(hp, old, c0, n),
                                                                 in1=spb[:, o:o + 2, c0:c0 + n], op=ALU.add),
                                reads=[("spsum", 2 * hp, old), ("spsum", 2 * hp + 1, old), ("spb", p % 2)],
                                writes=[("spsum", 2 * hp, 1 - old), ("spsum", 2 * hp + 1, 1 - old)])

                        def EBst(p):
                            kb, hp = steps[p]
                            c0, n, diag = geom(kb)
                            o = 2 * (p % 2)
                            bp = BPAIR[p % 2]
                            add("act", lambda e: e.activation(out=wb[:, o:o + 2, c0:c0 + n], in_=PPv(bp)[:, :, c0:c0 + n], func=AF.Exp),
                                reads=[("P", 2 * bp), ("P", 2 * bp + 1)], writes=[("wb", p % 2)])

                        def PVst(p):
                            kb, hp = steps[p]
                            c0, n, diag = geom(kb)
                            o = 2 * (p % 2)
                            cbk = CB[hp]
                            for e_ in range(2):
                                hh = 2 * hp + e_
                                h = 4 * hg + hh
                                ps_ = slice(64 * e_, 64 * e_ + 64)
                                fs = firstC[hh]
                                firstC[hh] = False
                                mm(P[cbk][ps_, c0:c0 + n], vsb[:, kb, 64 * h:64 * h + 64], wb[:, o + e_, c0:c0 + n], fs, kb == 0,
                                   reads=[("vsb", kb), ("wb", p % 2)], writes=[("P", cbk)], skip_group_check=True)

                        Zst(0)
                        EAst(0)
                        LNst(0)
                        if npair > 1:
                            Zst(1)
                        for p in range(npair):
                            if p + 1 < npair:
                                EAst(p + 1)
                            fill(NFILL)
                            if p + 2 < npair:
                                Zst(p + 2)
                            fill(NFILL2)
                            if p >= 1:
                                EBst(p - 1)
                            Bst(p)
                            if p % 2 == 1:
                                trickle_cast()
                            if p >= 1:
                                PVst(p - 1)
                            if p + 1 < npair:
                                LNst(p + 1)
                        EBst(npair - 1)
                        PVst(npair - 1)
                        for cc in range(2):
                            c = 2 * hg + cc
                            add("dve", lambda e, cc=cc, c=c, CB=CB: e.tensor_copy(out=osb[:, c, :], in_=P[CB[cc]][:, :]),
                                reads=[("P", CB[cc])], writes=[("osb", c)])

                if filler is not None:
                    S_.tag = 'mlp'
                    for _ in filler:
                        pass
                S_.tag = 'merge'
                if LVL >= 5:
                    for pc in range(4):
                        sA = begin_slab(10 + 3 * pc)
                        for cc in range(2):
                            c = 2 * pc + cc
                            bk = [0, 1, 2, 3] if c % 2 == 0 else [4, 5, 6, 7]
                            proj_fm(sA, (2 * cc) * 512, 4, osb, "osb", bk[0])
                            proj_fm(sA, (2 * cc + 1) * 512, 4, osw, "osw", bk[1])
                            sGc = begin_slab(10 + 3 * pc + 1 + cc, live=(1 if cc == 0 else 0))
                            proj_fm(sGc, 0, KC, hT, "hT", bk[2])
                            proj_fm(sGc, 1024, KC, hT, "hT", bk[3])
                            sgo = 2 * (c % 2)
                            add("act", lambda e, bk=bk, sgo=sgo: e.activation(out=scr[:, sgo, :], in_=P[bk[2]][:, :], func=AF.Sigmoid),
                                reads=[("P", bk[2])], writes=[("scr", sgo)])
                            add("act", lambda e, bk=bk, sgo=sgo: e.activation(out=scr[:, sgo + 1, :], in_=P[bk[3]][:, :], func=AF.Sigmoid),
                                reads=[("P", bk[3])], writes=[("scr", sgo + 1)])
                            rs = c % 2
                            add("dve", lambda e, bk=bk, sgo=sgo, rs=rs: e.tensor_tensor(out=scr[:, 4, :], in0=P[bk[0]][:, :], in1=scr[:, sgo, :], op=ALU.mult),
                                reads=[("P", bk[0]), ("scr", sgo)], writes=[("scr", 4)])
                            add("dve", lambda e, bk=bk, sgo=sgo, rs=rs: e.tensor_tensor(out=scr[:, 5, :], in0=P[bk[1]][:, :], in1=scr[:, sgo + 1, :], op=ALU.mult),
                                reads=[("P", bk[1]), ("scr", sgo + 1)], writes=[("scr", 5)])
                            add("dve", lambda e, rs=rs, c=c: e.tensor_tensor(out=qm[:, c, :], in0=scr[:, 4, :], in1=scr[:, 5, :], op=ALU.add),
                                reads=[("scr", 4), ("scr", 5)], writes=[("qm", c)])
                if gi + 1 < NTL:
                    emit_load(gi + 1, eng="sp")
                gn = gi + 1
                if gn < NTL:
                    S_.tag = 'n1'
                    xtn = xt2[:, (gn % 2) * KC:(gn % 2 + 1) * KC, :]
                    norm_sq(xtn, gn % 2)
                S_.tag = 'wout'
                if LVL >= 6:
                    for c in range(KC):
                        slot = begin_slab(22 + c // 2)
                        bank = pbank([0, 1, 2, 3])
                        proj_fm(slot, (c % 2) * 1024, KC, qm, "qm", bank)
                        add("dve", lambda e, c=c, bank=bank, xt=xt: e.tensor_tensor(out=xt[:, c, :], in0=P[bank][:, :], in1=xt[:, c, :], op=ALU.add),
                            reads=[("P", bank), ("xt", xp, c)], writes=[("xt", xp, c)])
                if gn < NTL:
                    S_.tag = 'n1'
                    norm_rest(gn // NT, 0, xtn, gn % 2, hT, "hT")
        gl = NTL - 1
        S_.tag = 'mlp'
        rmsnorm(gl // NT, 8, xt2[:, (gl % 2) * KC:(gl % 2 + 1) * KC, :], gl % 2, h2T, "h2T")
        for _ in mlp_gen(gl):
            pass
        add("sp", None, extra_deps=[S_.dma_last["xst0"], S_.dma_last["xst1"]])

        S_.finalize()
        with nc.Block() as block:
            @block.tensor
            def _(e):
                S_.replay("pe", e, esem, dsem)

            @block.scalar
            def _(e):
                S_.replay("act", e, esem, dsem)

            @block.vector
            def _(e):
                S_.replay("dve", e, esem, dsem)

            @block.gpsimd
            def _(e):
                S_.replay("pool", e, esem, dsem)

            @block.sync
            def _(e):
                S_.replay("sp", e, esem, dsem)
    nc._n_ops = len(S_.ops)
    nc._sched = S_
    return nc


def _fm_job(W, cols, kcn, k0=0):
    blk = W[k0 * 128:(k0 + kcn) * 128][:, cols].reshape(kcn, 128, len(cols))
    return np.ascontiguousarray(blk.transpose(1, 0, 2)).reshape(128, kcn * len(cols))


def pack_layer(w_in, w_bsb, w_bsw, w_out, w_up, w_down):
    slabs = np.zeros((NSLAB, 128, SLAB), np.float32)
    ar = np.arange

    def put(s, off, blk):
        slabs[s, :, off:off + blk.shape[1]] = blk

    for c in range(4):
        put(c // 2, (c % 2) * 1024, _fm_job(w_in, ar(c * 128, c * 128 + 128), 8))
        put(2 + c // 2, (c % 2) * 1024, _fm_job(w_in, 512 + ar(c * 128, c * 128 + 128), 8))
        put(4 + c // 2, (c % 2) * 1024, _fm_job(w_in, 1536 + ar(c * 128, c * 128 + 128), 8))
    for half in range(2):
        put(7 + half, 0, _fm_job(w_in, 1024 + ar(512), 4, k0=4 * half))
    for g in range(2):
        cols = 2048 + 64 * g + np.concatenate([ar(64), ar(64)])
        put(6, g * 1024, _fm_job(w_in, cols, 8))
    put(9, 0, _fm_job(w_in, 2176 + ar(128), 8))
    for pc in range(4):
        for cc in range(2):
            c = 2 * pc + cc
            put(10 + 3 * pc, (2 * cc) * 512, _fm_job(w_bsb, ar(c * 128, c * 128 + 128), 4))
            put(10 + 3 * pc, (2 * cc + 1) * 512, _fm_job(w_bsw, ar(c * 128, c * 128 + 128), 4))
            put(10 + 3 * pc + 1 + cc, 0, _fm_job(w_in, 2304 + ar(c * 128, c * 128 + 128), 8))
            put(10 + 3 * pc + 1 + cc, 1024, _fm_job(w_in, 2304 + 1024 + ar(c * 128, c * 128 + 128), 8))
    for c in range(8):
        put(22 + c // 2, (c % 2) * 1024, _fm_job(w_out, ar(c * 128, c * 128 + 128), 8))
    for hh in range(2):
        for j in range(16):
            jj = 16 * hh + j
            put(26 + 16 * hh + j // 2, (j % 2) * 1024, _fm_job(w_up, ar(jj * 128, jj * 128 + 128), 8))
        for c in range(8):
            put(26 + 16 * hh + 8 + c, 0, _fm_job(w_down, ar(c * 128, c * 128 + 128), 16, k0=16 * hh))
    return slabs


def make_consts():
    cst = np.zeros((128, NCST), np.float32)
    i = np.arange(128)
    cst[:, 0:128] = np.eye(128, dtype=np.float32)
    cst[:, 128:256] = -(i[:, None] >= i[None, :]).astype(np.float32)
    cst[:, 256:384] = -1.0
    cst[:, 384:512] = 1.0
    cst[:, 512:640] = ((i[:, None] // 64) == (i[None, :] // 64)).astype(np.float32)
    s_, t_ = i[:, None], i[None, :]
    cst[:, 640:768] = np.where(s_ < t_, 0.0, NEG)
    mC = np.where(s_ <= t_, 0.0, NEG)
    mP = np.where(s_ > t_, 0.0, NEG)
    m = np.concatenate([mC, mP], axis=1)
    cst[:, 768:1024] = m
    cst[:, 1024:1280] = m
    RT = np.zeros((128, 128), np.float32)
    for d in range(128):
        hd, dd = divmod(d, 64)
        if dd < 32:
            RT[hd * 64 + dd + 32, d] = -1.0
        else:
            RT[hd * 64 + dd - 32, d] = 1.0
    cst[:, NCB:NCB + 128] = RT
    return cst


def make_rope(S):
    inv_freq = (1.0 / (np.float32(10000.0) ** (np.arange(0, 64, 2, dtype=np.float32) / np.float32(64)))).astype(np.float32)
    ang = np.arange(S, dtype=np.float32)[:, None] * inv_freq[None, :]
    cos, sin = np.cos(ang).astype(np.float32), np.sin(ang).astype(np.float32)
    f = (np.arange(128) % 64) % 32
    return np.ascontiguousarray(np.stack([cos[:, f].T, sin[:, f].T], 0))


def make_gv(mix_norm_g, mlp_norm_g, q_norm_g, k_norm_g, sinks):
    depth = mix_norm_g.shape[0]
    gvv = np.zeros((128, depth * NGV), np.float32)
    p = np.arange(128)
    for l in range(depth):
        o = l * NGV
        gvv[:, o:o + 8] = mix_norm_g[l].reshape(8, 128).T
        gvv[:, o + 8:o + 16] = mlp_norm_g[l].reshape(8, 128).T
        gvv[:, o + 16] = q_norm_g[l][p % 64]
        gvv[:, o + 17] = k_norm_g[l][p % 64]
        for c in range(4):
            gvv[:, o + 18 + c] = sinks[l][2 * c + p // 64]
    return gvv


_CACHE = {}


def _get_nc(S, depth):
    key = (S, depth)
    if key not in _CACHE:
        _CACHE[key] = build(S, depth)
    return _CACHE[key]


def kernel(x, mix_norm_g, w_in, q_norm_g, k_norm_g, sinks, w_branch_sb, w_branch_swa, w_out, mlp_norm_g, w_up, w_down):
    x = np.asarray(x, np.float32)
    B, S, _ = x.shape
    depth = int(np.asarray(w_in).shape[0])
    f = lambda a: np.asarray(a, np.float32)
    wsl = np.concatenate([pack_layer(f(w_in[l]), f(w_branch_sb[l]), f(w_branch_swa[l]), f(w_out[l]), f(w_up[l]), f(w_down[l]))
                          for l in range(depth)], 0)
    gvv = make_gv(f(mix_norm_g), f(mlp_norm_g), f(q_norm_g), f(k_norm_g), f(sinks))
    cst = make_consts()
    rope = make_rope(S)
    nc = _get_nc(S, depth)
    in_maps = []
    for b in range(B):
        xT = np.ascontiguousarray(x[b].T).reshape(KC, 128, S)
        in_maps.append({"xT": xT, "wsl": wsl, "gv": gvv, "cst": cst, "rope": rope})
    res = run_bass_kernel_spmd(nc, in_maps, core_ids=list(range(B)))
    out = np.empty((B, S, D), np.float32)
    for b in range(B):
        out[b] = res.results[b]["yT"].reshape(D, S).T
    return out
```

```python
import os
import numpy as np
import concourse.bass as bass
import concourse.mybir as mybir
from concourse.bass_utils import run_bass_kernel_spmd
from contextlib import ExitStack

F32, BF16 = mybir.dt.float32, mybir.dt.bfloat16
AF = mybir.ActivationFunctionType
ALU = mybir.AluOpType

D = 1024
KC = 8
T = 512
DFF = 4096
SLAB = 2048
NSLOT = 5
NSLAB = 58
NGV = 22
NCB = 1280
NCST = NCB + 128
EPS = 1e-6
LVL = float(os.environ.get('K_LVL', '99'))
NFILL = int(os.environ.get('K_NFILL', '2'))
NFILL2 = int(os.environ.get('K_NFILL2', '2'))
NEG = -30000.0

ENGS = ["pe", "act", "dve", "pool", "sp"]


class Sched:
    def __init__(self):
        self.ops = []
        self.q = {e: [] for e in ENGS}
        self.last_w = {}
        self.readers = {}
        self.dma_last = {}
        self.tag = ''

    def add(self, eng, fn, reads=(), writes=(), dma=None, extra_deps=()):
        op = dict(id=len(self.ops), eng=eng, fn=fn, deps=set(extra_deps), dma=dma, sig=False, val=None, tag=self.tag)
        preads = [r for r in reads if isinstance(r, tuple) and r[0] == "P"]
        if preads:
            reads = [r for r in reads if r not in preads]
            writes = list(writes) + preads
        for r in reads:
            if r in self.last_w:
                op["deps"].add(self.last_w[r])
        for w in writes:
            if w in self.last_w:
                op["deps"].add(self.last_w[w])
            for t in self.readers.get(w, {}).values():
                op["deps"].add(t)
        if dma is not None and dma in self.dma_last:
            op["deps"].add(self.dma_last[dma])
        key = eng if dma is None else ("dma", dma)
        for r in reads:
            self.readers.setdefault(r, {})[key] = op["id"]
        for w in writes:
            self.last_w[w] = op["id"]
            self.readers[w] = {}
        if dma is not None:
            self.dma_last[dma] = op["id"]
        op["deps"].discard(op["id"])
        self.ops.append(op)
        self.q[eng].append(op)
        return op["id"]

    def needs_wait(self, a, b):
        if a["dma"] is not None:
            return True
        if a["eng"] == b["eng"]:
            return a["eng"] != "pe"
        return True

    def finalize(self):
        for b in self.ops:
            for ai in b["deps"]:
                a = self.ops[ai]
                if self.needs_wait(a, b):
                    a["sig"] = True
        cnt = {e: 0 for e in ENGS}
        dcnt = {}
        for op in self.ops:
            if op["dma"] is not None:
                dcnt[op["dma"]] = dcnt.get(op["dma"], 0) + 16
                op["val"] = dcnt[op["dma"]]
            elif op["sig"]:
                cnt[op["eng"]] += 1
                op["val"] = cnt[op["eng"]]

    def replay(self, eng, e, esem, dsem):
        waited = {}
        for op in self.q[eng]:
            need = {}
            for ai in op["deps"]:
                a = self.ops[ai]
                if not self.needs_wait(a, op):
                    continue
                s = ("d", a["dma"]) if a["dma"] is not None else ("e", a["eng"])
                need[s] = max(need.get(s, 0), a["val"])
            for s, v in need.items():
                if waited.get(s, 0) >= v:
                    continue
                waited[s] = v
                h = dsem[s[1]] if s[0] == "d" else esem[s[1]]
                e.wait_ge(h, v)
            if op["fn"] is None:
                continue
            ins = op["fn"](e)
            if op["dma"] is not None:
                ins.then_inc(dsem[op["dma"]], 16)
            elif op["sig"]:
                ins.then_inc(esem[eng], 1)


def build(S, depth):
    NT = S // T
    NB = S // 128
    nc = bass.Bass("TRN2", target_bir_lowering=False)
    xT_d = nc.dram_tensor("xT", [KC, 128, S], F32, kind="ExternalInput").ap()
    wsl_d = nc.dram_tensor("wsl", [depth * NSLAB, 128, SLAB], F32, kind="ExternalInput").ap()
    gv_d = nc.dram_tensor("gv", [128, depth * NGV], F32, kind="ExternalInput").ap()
    cst_d = nc.dram_tensor("cst", [128, NCST], F32, kind="ExternalInput").ap()
    rope_d = nc.dram_tensor("rope", [2, 128, S], F32, kind="ExternalInput").ap()
    yT_d = nc.dram_tensor("yT", [KC, 128, S], F32, kind="ExternalOutput").ap()
    wbf_d = nc.dram_tensor("wbf", [depth * NSLAB, 128, SLAB], BF16).ap()
    xres_d = nc.dram_tensor("xres", [KC, 128, S], F32).ap()

    sb_specs = [
        ("ksb", [128, 4, S], BF16), ("vsb", [128, NB, 512], BF16),
        ("kw", [128, 2, 640], BF16), ("vw", [128, 5, 128], BF16),
        ("xt2", [128, 2 * KC, T], F32), ("hT", [128, KC, T], BF16), ("h2T", [128, KC, T], BF16),
        ("qm", [128, KC, T], BF16),
        ("osb", [128, 4, T], BF16), ("osw", [128, 4, T], BF16),
        ("uT", [128, 16, T], BF16),
        ("ring", [128, NSLOT, SLAB], BF16),
        ("ebuf", [128, 2, T], F32), ("spb", [128, 4, T], BF16), ("wb", [128, 4, T], BF16),
        ("spsum", [128, 8 * T], BF16),
        ("rstd", [128, T], F32),
        ("scr", [128, 6, T], F32),
        ("cb", [128, NCB], BF16), ("rt", [128, 128], F32),
        ("gv", [128, depth * NGV], F32), ("gq8", [128, depth], F32), ("esink", [128, depth * 4], F32),
    ]
    S_ = Sched()
    add = S_.add

    with ExitStack() as es:
        sb = {}
        for name, shape, dt in sb_specs:
            sb[name] = es.enter_context(nc.sbuf_tensor("s_" + name, shape, dt))
        PP = [es.enter_context(nc.psum_tensor(f"PP{i}", [128, 1024], F32)) for i in range(4)]

        class _Banks:
            def __getitem__(self, i):
                return PP[i // 2][:, (i % 2) * 512:(i % 2) * 512 + 512]

        P = _Banks()

        def PPv(k):
            return PP[k][:, :].rearrange("p (e c) -> p e c", e=2)

        esem = {e: es.enter_context(nc.semaphore(f"se_{e}")) for e in ENGS}
        dnames = [f"ring{i}" for i in range(NSLOT)] + ["xld0", "xld1", "xldf", "xst0", "xst1", "rope", "cst", "gv", "rt"] + [f"cast{i}" for i in range(4)]
        dsem = {n: es.enter_context(nc.semaphore(f"sd_{n}")) for n in dnames}

        ksb, vsb, kw, vw, hT = sb["ksb"], sb["vsb"], sb["kw"], sb["vw"], sb["hT"]
        xt2 = sb["xt2"]
        h2T = sb["h2T"]

        def sqv(sl):
            return sb["spb"][:, 2 * sl, :]

        def pTv(k):
            return sb["wb"][:, 2 * k:2 * k + 2, :].rearrange("p a c -> p (a c)")

        qm, osb, osw, uT, ring = sb["qm"], sb["osb"], sb["osw"], sb["uT"], sb["ring"]
        ebuf, spb, wb, spsum, scr = sb["ebuf"], sb["spb"], sb["wb"], sb["spsum"], sb["scr"]
        cb, gv = sb["cb"], sb["gv"]
        ident = cb[:, 0:128]
        negtri = cb[:, 128:256]
        negones = cb[:, 256:384]
        ones = cb[:, 384:512]
        bd = cb[:, 512:640]
        sbmask = cb[:, 640:768]
        RT = sb["rt"][:, :]

        def v3(ap, h):
            return ap.rearrange("p (h c) -> p h c", h=h)

        def sps(hh, pp, c0, n):
            o = (hh * 2 + pp) * T
            return spsum[:, o + c0:o + c0 + n]

        def swm(h0, c0, c1):
            return v3(cb[:, 768:1280], 2)[:, :, c0:c1]

        add("pool", lambda e: e.dma_start(out=cb[:, :], in_=cst_d[:, 0:NCB]), writes=["cb"], dma="cst")
        add("sp", lambda e: e.dma_start(out=sb["rt"][:, :], in_=cst_d[:, NCB:NCB + 128]), writes=["cstf"], dma="rt")
        add("sp", lambda e: e.dma_start(out=gv[:, :], in_=gv_d[:, :]), writes=["gv"], dma="gv")
        for l in range(depth):
            add("dve", lambda e, l=l: e.tensor_scalar(out=sb["gq8"][:, l:l + 1], in0=gv[:, l * NGV + 16:l * NGV + 17],
                                                      scalar1=0.125, scalar2=None, op0=ALU.mult),
                reads=["gv"], writes=[("gq8", l)])
            add("act", lambda e, l=l: e.activation(out=sb["esink"][:, 4 * l:4 * l + 4], in_=gv[:, l * NGV + 18:l * NGV + 22],
                                                   func=AF.Exp), reads=["gv"], writes=[("esink", l)])
        cst_state = dict(ci=0)

        def emit_cast(l, extra=()):
            bounds = [0, 2, 6, 14, 22, 30, 38, 46, 54, NSLAB]
            for gidx, (g0, g1) in enumerate(zip(bounds, bounds[1:])):
                a0, a1 = l * NSLAB + g0, l * NSLAB + g1
                add("pool", lambda e, a0=a0, a1=a1: e.dma_start(out=wbf_d[a0:a1], in_=wsl_d[a0:a1]),
                    writes=[("wbf", l, s) for s in range(g0, g1)], dma=f"cast{cst_state['ci'] % 4}",
                    extra_deps=(extra if gidx >= 1 else ()))
                cst_state["ci"] += 1

        cast_pending = []

        def trickle_cast():
            if cast_pending:
                l_, s_ = cast_pending.pop(0)
                a0 = l_ * NSLAB + s_
                add("pool", lambda e: e.dma_start(out=wbf_d[a0:a0 + 1], in_=wsl_d[a0:a0 + 1]),
                    writes=[("wbf", l_, s_)], dma=f"cast{cst_state['ci'] % 4}", extra_deps=[len(S_.ops) - 1])
                cst_state["ci"] += 1

        st = dict(next_dma=0, gbase=0, layer=0)

        def slab_region(g):
            return ("ring", g % NSLOT)

        NTL = depth * NT
        order = []
        for gi_ in range(NTL):
            l_ = gi_ // NT
            order += [(l_, n_) for n_ in range(0, 10)]
            if gi_ > 0:
                order += [((gi_ - 1) // NT, n_) for n_ in range(26, NSLAB)]
            order += [(l_, n_) for n_ in range(10, 26)]
        order += [(depth - 1, n_) for n_ in range(26, NSLAB)]
        st["pos"] = -1

        def begin_slab(n, layer=None, live=0):
            tgt = (st["layer"] if layer is None else layer, n)
            if st["pos"] < 0 or order[st["pos"]] != tgt:
                st["pos"] += 1
                assert order[st["pos"]] == tgt, (order[st["pos"]], tgt)
            g = st["pos"]
            upto = min(g + NSLOT - live, len(order))
            while st["next_dma"] < upto:
                gg = st["next_dma"]
                ll, nn = order[gg]
                slot = gg % NSLOT
                add("sp", lambda e, slot=slot, src=ll * NSLAB + nn: e.dma_start(out=ring[:, slot, :], in_=wbf_d[src]),
                    reads=[("wbf", ll, nn)], writes=[("ring", slot)], dma=f"ring{slot}")
                st["next_dma"] += 1
            return g % NSLOT

        def wfm(slot, off, kcn):
            return ring[:, slot, off:off + kcn * 128].rearrange("p (k m) -> p k m", k=kcn)

        pb = dict(i=0)

        def pbank(lst):
            pb["i"] += 1
            return lst[pb["i"] % len(lst)]

        MMF = {
            "n1": lambda o, l_, r, a_, b_, k: (lambda e: e.matmul(o, lhsT=l_, rhs=r, start=a_, stop=b_, **k)),
            "proj": lambda o, l_, r, a_, b_, k: (lambda e: e.matmul(o, lhsT=l_, rhs=r, start=a_, stop=b_, **k)),
            "swa": lambda o, l_, r, a_, b_, k: (lambda e: e.matmul(o, lhsT=l_, rhs=r, start=a_, stop=b_, **k)),
            "sb": lambda o, l_, r, a_, b_, k: (lambda e: e.matmul(o, lhsT=l_, rhs=r, start=a_, stop=b_, **k)),
            "merge": lambda o, l_, r, a_, b_, k: (lambda e: e.matmul(o, lhsT=l_, rhs=r, start=a_, stop=b_, **k)),
            "wout": lambda o, l_, r, a_, b_, k: (lambda e: e.matmul(o, lhsT=l_, rhs=r, start=a_, stop=b_, **k)),
            "mlp": lambda o, l_, r, a_, b_, k: (lambda e: e.matmul(o, lhsT=l_, rhs=r, start=a_, stop=b_, **k)),
        }

        def mm(out, lhsT, rhs, start, stop, reads, writes, **kw_):
            i = add("pe", MMF[S_.tag](out, lhsT, rhs, start, stop, kw_), reads=reads, writes=writes)
            n = 1
            for d in rhs.shape[1:]:
                n *= d
            S_.ops[i]["shape"] = "%d*%d*%d" % (lhsT.shape[0], lhsT.shape[1], n)
            return i

        def rmsnorm(l, gcol, xt, xp, dst, dreg):
            bank = 6
            for kc in range(KC):
                sl = kc % 2
                add("act", lambda e, kc=kc, sl=sl: e.activation(out=sqv(sl), in_=xt[:, kc, :], func=AF.Square),
                    reads=[("xt", xp, kc)], writes=[("spb", sl)])
                mm(P[bank][:, :], ones, sqv(sl), kc == 0, kc == KC - 1,
                   reads=["cb", ("spb", sl)], writes=[("P", bank)])
            add("act", lambda e: e.activation(out=sb["rstd"][:, :], in_=P[bank][:, :], func=AF.Ln, scale=1.0 / D, bias=EPS),
                reads=[("P", bank)], writes=["rstd"])
            add("act", lambda e: e.activation(out=sb["rstd"][:, :], in_=sb["rstd"][:, :], func=AF.Exp, scale=-0.5),
                reads=["rstd"], writes=["rstd"])
            for kc in range(KC):
                add("dve", lambda e, kc=kc: e.scalar_tensor_tensor(out=dst[:, kc, :], in0=xt[:, kc, :],
                                                                   scalar=gv[:, l * NGV + gcol + kc:l * NGV + gcol + kc + 1],
                                                                   in1=sb["rstd"][:, :], op0=ALU.mult, op1=ALU.mult),
                    reads=[("xt", xp, kc), "rstd", "gv"], writes=[(dreg, kc)])

        def norm_sq(xt, xp):
            for kc in range(KC):
                add("act", lambda e, kc=kc: e.activation(out=uT[:, kc, :], in_=xt[:, kc, :], func=AF.Square),
                    reads=[("xt", xp, kc)], writes=[("uT", kc)])

        def norm_rest(l, gcol, xt, xp, dst, dreg):
            bank = 6
            for kc in range(KC):
                mm(P[bank][:, :], ones, uT[:, kc, :], kc == 0, kc == KC - 1,
                   reads=["cb", ("uT", kc)], writes=[("P", bank)])
            add("act", lambda e: e.activation(out=sb["rstd"][:, :], in_=P[bank][:, :], func=AF.Ln, scale=1.0 / D, bias=EPS),
                reads=[("P", bank)], writes=["rstd"])
            add("act", lambda e: e.activation(out=sb["rstd"][:, :], in_=sb["rstd"][:, :], func=AF.Exp, scale=-0.5),
                reads=["rstd"], writes=["rstd"])
            for kc in range(KC):
                add("dve", lambda e, kc=kc: e.scalar_tensor_tensor(out=dst[:, kc, :], in0=xt[:, kc, :],
                                                                   scalar=gv[:, l * NGV + gcol + kc:l * NGV + gcol + kc + 1],
                                                                   in1=sb["rstd"][:, :], op0=ALU.mult, op1=ALU.mult),
                    reads=[("xt", xp, kc), "rstd", "gv"], writes=[(dreg, kc)])

        def proj_fm(slot, off, kcn, src, src_reg, bank):
            w = wfm(slot, off, kcn)
            for kc in range(kcn):
                mm(P[bank][:, :], w[:, kc, :], src[:, kc, :], kc == 0, kc == kcn - 1,
                   reads=[("ring", slot), (src_reg, kc)], writes=[("P", bank)])

        def emit_load(gi, eng="pool"):
            l_, ti_ = divmod(gi, NT)
            xp_ = gi % 2
            src = xT_d if l_ == 0 else xres_d
            t0_ = ti_ * T
            return add(eng, lambda e: e.dma_start(out=xt2[:, xp_ * KC:(xp_ + 1) * KC, :], in_=src.rearrange("k p s -> p k s")[:, :, t0_:t0_ + T]),
                       reads=[("xres", ti_)] if l_ > 0 else [], writes=[("xt", xp_, kc) for kc in range(KC)],
                       dma=f"xld{xp_}")

        def mlp_gen(gp):
            lp, tip = divmod(gp, NT)
            xpp = gp % 2
            xtp = xt2[:, xpp * KC:(xpp + 1) * KC, :]
            MB = [6, 7]
            for hh in range(2):
                for j in range(16):
                    slot = begin_slab(26 + 16 * hh + j // 2, layer=lp)
                    bank = pbank(MB)
                    w = wfm(slot, (j % 2) * 1024, KC)
                    rs = j % 2
                    for kc in range(KC):
                        mm(P[bank][:, :], w[:, kc, :], h2T[:, kc, :], kc == 0, kc == KC - 1,
                           reads=[("ring", slot), ("h2T", kc)], writes=[("P", bank)])
                        if kc == KC - 1:
                            add("dve", lambda e, bank=bank, rs=rs: e.tensor_scalar(out=scr[:, rs, :], in0=P[bank][:, :], scalar1=0.0, scalar2=None, op0=ALU.max),
                                reads=[("P", bank)], writes=[("scr", rs)])
                            add("dve", lambda e, rs=rs, j=j: e.tensor_tensor(out=uT[:, j, :], in0=scr[:, rs, :], in1=scr[:, rs, :], op=ALU.mult),
                                reads=[("scr", rs)], writes=[("uT", j)])
                        yield
                for c in range(KC):
                    slot = begin_slab(26 + 16 * hh + 8 + c, layer=lp)
                    bank = pbank(MB)
                    w = wfm(slot, 0, 16)
                    for kc in range(16):
                        mm(P[bank][:, :], w[:, kc, :], uT[:, kc, :], kc == 0, kc == 15,
                           reads=[("ring", slot), ("uT", kc)], writes=[("P", bank)])
                        if kc == 15:
                            add("dve", lambda e, c=c, bank=bank: e.tensor_tensor(out=xtp[:, c, :], in0=P[bank][:, :], in1=xtp[:, c, :], op=ALU.add),
                                reads=[("P", bank), ("xt", xpp, c)], writes=[("xt", xpp, c)])
                        yield
            if tip == NT - 1:
                while cast_pending:
                    trickle_cast()
            dst = yT_d if lp == depth - 1 else xres_d
            add("sp", lambda e: e.dma_start(out=dst.rearrange("k p s -> p k s")[:, :, tip * T:(tip + 1) * T], in_=xtp),
                reads=[("xt", xpp, kc) for kc in range(KC)], writes=[("xres", tip)] if lp < depth - 1 else [("yT", tip)], dma=f"xst{xpp}")

        ld0 = emit_load(0, eng="sp")
        emit_cast(0, extra=[ld0])
        for l in range(depth):
            for ti in range(NT):
                t0 = ti * T
                b0 = ti * 4
                gi = l * NT + ti
                xp = gi % 2
                xt = xt2[:, xp * KC:(xp + 1) * KC, :]
                st["layer"] = l
                last_layer = (l == depth - 1)
                S_.tag = 'load'
                if ti == 0 and l + 1 < depth:
                    cast_pending.extend((l + 1, s_) for s_ in range(NSLAB))
                add("sp", lambda e, t0=t0: e.dma_start(out=ebuf[:, :, :], in_=rope_d.rearrange("c p s -> p c s")[:, :, t0:t0 + T]),
                    writes=["ebuf"], dma="rope")
                S_.tag = 'n1'
                if gi == 0:
                    rmsnorm(l, 0, xt, xp, hT, "hT")
                S_.tag = 'proj'
                if LVL >= 2:
                    PB = [0, 1, 2, 3, 4, 5, 6, 7]
                    for c in range(4):
                        slot = begin_slab(c // 2)
                        bank = pbank(PB)
                        proj_fm(slot, (c % 2) * 1024, KC, hT, "hT", bank)
                        add("dve", lambda e, c=c, bank=bank: e.tensor_scalar(out=qm[:, c, :], in0=P[bank][:, :], scalar1=0.125, scalar2=None, op0=ALU.mult),
                            reads=[("P", bank)], writes=[("qm", c)])
                    for c in range(4 if LVL >= 2.2 else 0):
                        slot = begin_slab(2 + c // 2)
                        bank = pbank(PB)
                        proj_fm(slot, (c % 2) * 1024, KC, hT, "hT", bank)
                        add("dve", lambda e, c=c, bank=bank, t0=t0: e.tensor_copy(out=ksb[:, c, t0:t0 + T], in_=P[bank][:, :]),
                            reads=[("P", bank)], writes=[("ksb", c, ti)])
                    chb = {}

                    def chA(c):
                        slot = begin_slab(4 + c // 2)
                        bank = pbank(PB)
                        proj_fm(slot, (c % 2) * 1024, KC, hT, "hT", bank)
                        rs = c % 2
                        add("dve", lambda e: e.tensor_copy(out=scr[:, rs, :], in_=P[bank][:, :]),
                            reads=[("P", bank)], writes=[("scr", rs)])
                        add("act", lambda e: e.activation(out=sqv(rs), in_=P[bank][:, :], func=AF.Square),
                            reads=[("P", bank)], writes=[("spb", rs)])

                    def chB(c):
                        rs = c % 2
                        bank2 = pbank(PB)
                        mm(P[bank2][:, :], bd, sqv(rs), True, True, reads=["cb", ("spb", rs)], writes=[("P", bank2)])
                        add("act", lambda e: e.activation(out=sb["rstd"][:, :], in_=P[bank2][:, :], func=AF.Ln, scale=1.0 / 64, bias=EPS),
                            reads=[("P", bank2)], writes=["rstd"])
                        add("act", lambda e: e.activation(out=sb["rstd"][:, :], in_=sb["rstd"][:, :], func=AF.Exp, scale=-0.5),
                            reads=["rstd"], writes=["rstd"])
                        gcolap = sb["gq8"][:, l:l + 1] if c < 4 else gv[:, l * NGV + 17:l * NGV + 18]
                        add("dve", lambda e: e.scalar_tensor_tensor(out=scr[:, 2 + rs, :], in0=scr[:, rs, :], scalar=gcolap,
                                                                    in1=sb["rstd"][:, :], op0=ALU.mult, op1=ALU.mult),
                            reads=[("scr", rs), "rstd", "gv", ("gq8", l)], writes=[("scr", 2 + rs)])

                    def chC(c):
                        rs = c % 2
                        bank3 = pbank(PB)
                        mm(P[bank3][:, :], RT, scr[:, 2 + rs, :], True, True, reads=["cstf", ("scr", 2 + rs)], writes=[("P", bank3)])
                        add("dve", lambda e: e.tensor_tensor(out=scr[:, 4, :], in0=scr[:, 2 + rs, :], in1=ebuf[:, 0, :], op=ALU.mult),
                            reads=[("scr", 2 + rs), "ebuf"], writes=[("scr", 4)])
                        add("dve", lambda e: e.tensor_tensor(out=scr[:, 5, :], in0=P[bank3][:, :], in1=ebuf[:, 1, :], op=ALU.mult),
                            reads=[("P", bank3), "ebuf"], writes=[("scr", 5)])
                        if c < 4:
                            dst = qm[:, 4 + c, :]
                            dreg = ("qm", 4 + c)
                        else:
                            dst = kw[:, c - 4, 128:640]
                            dreg = ("kw", c - 4, 1)
                        add("dve", lambda e: e.tensor_tensor(out=dst, in0=scr[:, 4, :], in1=scr[:, 5, :], op=ALU.add),
                            reads=[("scr", 4), ("scr", 5)], writes=[dreg])

                    def vsb_tb(tb, s7, s8):
                        bank = pbank(PB)
                        for kc in range(KC):
                            slot = s7 if kc < 4 else s8
                            wv = ring[:, slot, :].rearrange("p (k n) -> p k n", k=4)[:, kc % 4, :]
                            mm(P[bank][:, :], hT[:, kc, tb * 128:(tb + 1) * 128], wv, kc == 0, kc == KC - 1,
                               reads=[("ring", slot), ("hT", kc)], writes=[("P", bank)])
                        blk = b0 + tb
                        add("dve", lambda e: e.tensor_copy(out=vsb[:, blk, :], in_=P[bank][:, :]),
                            reads=[("P", bank)], writes=[("vsb", blk)])

                    chA(0)
                    chA(1)
                    chB(0)
                    chA(2)
                    chB(1)
                    chC(0)
                    chA(3)
                    chB(2)
                    chC(1)
                    chA(4)
                    chB(3)
                    chC(2)
                    chA(5)
                    chB(4)
                    chC(3)
                    s7 = begin_slab(7)
                    s8 = begin_slab(8, live=1)
                    vsb_tb(0, s7, s8)
                    chB(5)
                    chC(4)
                    vsb_tb(1, s7, s8)
                    chC(5)
                    vsb_tb(2, s7, s8)
                    vsb_tb(3, s7, s8)
                    s9 = begin_slab(9)
                    for tb in range(4):
                        bank = pbank(PB)
                        wv = ring[:, s9, 0:1024].rearrange("p (k n) -> p k n", k=8)
                        for kc in range(KC):
                            mm(P[bank][:, 0:128], hT[:, kc, tb * 128:(tb + 1) * 128], wv[:, kc, :], kc == 0, kc == KC - 1,
                               reads=[("ring", s9), ("hT", kc)], writes=[("P", bank)])
                        add("dve", lambda e, bank=bank, tb=tb: e.tensor_copy(out=vw[:, 1 + tb, :], in_=P[bank][:, 0:128]),
                            reads=[("P", bank)], writes=[("vw", 1 + tb)])

                S_.tag = 'swa'
                if LVL >= 3:
                    for g in range(2):
                        OB = [0, 1] if g == 0 else [6, 7]
                        DB = [2, 3]
                        SBK = [4, 5]
                        units = []
                        for kbr in range(-1 if ti > 0 else 0, 4):
                            for par in range(2):
                                units.append((kbr, par))
                        first = {(ci_, par): True for ci_ in range(2) for par in range(2)}

                        def ugeom(kbr):
                            qlo = max(kbr, 0)
                            qhi = min(kbr + 1, 3)
                            return qlo * 128, (qhi - qlo + 1) * 128, (0 if kbr >= 0 else 128)

                        def swS(i):
                            kbr, par = units[i]
                            c0, ncol, m0 = ugeom(kbr)
                            ps_ = slice(64 * par, 64 * par + 64)
                            sbk = SBK[i % 2]
                            kcols = slice((kbr + 1) * 128, (kbr + 2) * 128)
                            mm(v3(P[sbk][:, 0:2 * ncol], 2), kw[ps_, g, kcols], qm[ps_, 4 + 2 * g:4 + 2 * g + 2, c0:c0 + ncol], True, False,
                               reads=[("kw", g, 0 if kbr < 0 else 1)] + [("qm", 4 + 2 * g), ("qm", 4 + 2 * g + 1)], writes=[("P", sbk)])
                            mm(v3(P[sbk][:, 0:2 * ncol], 2), ident, swm(0, m0, m0 + ncol), False, True, reads=["cb"], writes=[("P", sbk)])

                        def swE(i):
                            kbr, par = units[i]
                            c0, ncol, m0 = ugeom(kbr)
                            sbk = SBK[i % 2]
                            add("act", lambda e: e.activation(out=pTv(i % 2)[:, 0:2 * ncol], in_=P[sbk][:, 0:2 * ncol], func=AF.Exp),
                                reads=[("P", sbk)], writes=[("wb", i % 2)])

                        def swPV(i):
                            kbr, par = units[i]
                            c0, ncol, m0 = ugeom(kbr)
                            ps_ = slice(64 * par, 64 * par + 64)
                            for ci_ in range(2):
                                rhs = pTv(i % 2)[:, ci_ * ncol:(ci_ + 1) * ncol]
                                fs = first[(ci_, par)]
                                first[(ci_, par)] = False
                                mm(P[OB[ci_]][ps_, c0:c0 + ncol], vw[:, kbr + 1, 64 * g:64 * g + 64], rhs, fs, False,
                                   reads=[("vw", kbr + 1), ("wb", i % 2)], writes=[("P", OB[ci_])], skip_group_check=True)
                                mm(P[DB[ci_]][ps_, c0:c0 + ncol], ones[:, 0:64], rhs, fs, False,
                                   reads=["cb", ("wb", i % 2)], writes=[("P", DB[ci_])], skip_group_check=True)

                        swS(0)
                        for i in range(len(units)):
                            if i + 1 < len(units):
                                swS(i + 1)
                            swE(i)
                            swPV(i)
                        for ci_ in range(2):
                            c = 2 * g + ci_
                            add("dve", lambda e, ci_=ci_, c=c, l=l, DB=DB: e.tensor_scalar(out=scr[:, 2 + ci_, :], in0=P[DB[ci_]][:, :], scalar1=sb["esink"][:, 4 * l + c:4 * l + c + 1],
                                                                              scalar2=None, op0=ALU.add),
                                reads=[("P", DB[ci_]), ("esink", l)], writes=[("scr", 2 + ci_)])
                        for ci_ in range(2):
                            c = 2 * g + ci_
                            add("dve", lambda e, ci_=ci_: e.reciprocal(out=scr[:, 2 + ci_, :], in_=scr[:, 2 + ci_, :]),
                                reads=[("scr", 2 + ci_)], writes=[("scr", 2 + ci_)])
                            add("dve", lambda e, ci_=ci_, c=c, OB=OB: e.tensor_tensor(out=osw[:, c, :], in0=P[OB[ci_]][:, :], in1=scr[:, 2 + ci_, :], op=ALU.mult),
                                reads=[("P", OB[ci_]), ("scr", 2 + ci_)], writes=[("osw", c)])
                    if ti < NT - 1:
                        for g in range(2):
                            add("dve", lambda e, g=g: e.tensor_copy(out=kw[:, g, 0:128], in_=kw[:, g, 512:640]),
                                reads=[("kw", g, 1)], writes=[("kw", g, 0)])
                        add("dve", lambda e: e.tensor_copy(out=vw[:, 0, :], in_=vw[:, 4, :]), reads=[("vw", 4)], writes=[("vw", 0)])

                filler = None
                if gi > 0:
                    S_.tag = 'mlp'
                    gpv = gi - 1
                    rmsnorm(gpv // NT, 8, xt2[:, (gpv % 2) * KC:(gpv % 2 + 1) * KC, :], gpv % 2, h2T, "h2T")
                    filler = mlp_gen(gpv)

                def fill(k):
                    nonlocal_f = filler
                    if nonlocal_f is None:
                        return
                    S_.tag = 'mlp'
                    for _ in range(k):
                        try:
                            next(nonlocal_f)
                        except StopIteration:
                            break
                    S_.tag = 'sb'

                S_.tag = 'sb'
                if LVL >= 4:
                    for hg in range(2):
                        CB = [2, 3]
                        BPAIR = [0, 0]
                        APAIR = 2
                        add("dve", lambda e: e.memset(spsum[:, :], 0.0), writes=[("spsum", hh, pp) for hh in range(4) for pp in range(2)])
                        steps = []
                        for kb in range(b0 + 3, -1, -1):
                            for hp in range(2):
                                steps.append((kb, hp))
                        npair = len(steps)
                        firstC = {hh: True for hh in range(4)}

                        def geom(kb):
                            j = max(kb - b0, 0)
                            return j * 128, T - j * 128, kb >= b0

                        def ebsel(p):
                            if p % 2 == 0:
                                return ebuf, ["ebuf"]
                            return scr[:, 4:6, :], [("scr", 4), ("scr", 5)]

                        def spair(hp, pp, c0, n):
                            return spsum[:, 4 * hp * T:(4 * hp + 4) * T].rearrange("p (e q c) -> p e q c", e=2, q=2)[:, :, pp, c0:c0 + n]

                        def Zst(p):
                            kb, hp = steps[p]
                            ch = 2 * hg + hp
                            c0, n, diag = geom(kb)
                            for e_ in range(2):
                                ps_ = slice(64 * e_, 64 * e_ + 64)
                                a = 2 * APAIR + e_
                                mm(P[a][:, c0:c0 + n], ksb[ps_, ch, kb * 128:(kb + 1) * 128], qm[ps_, ch, c0:c0 + n], True, not diag,
                                   reads=[("ksb", ch, kb // 4), ("qm", ch)], writes=[("P", a)])
                            if diag:
                                for e_ in range(2):
                                    a = 2 * APAIR + e_
                                    mm(P[a][:, c0:c0 + 128], ident, sbmask, False, True, reads=["cb"], writes=[("P", a)])

                        def EAst(p):
                            kb, hp = steps[p]
                            c0, n, diag = geom(kb)
                            eb, ebr = ebsel(p)
                            add("act", lambda e: e.activation(out=eb[:, :, c0:c0 + n], in_=PPv(APAIR)[:, :, c0:c0 + n], func=AF.Exp),
                                reads=[("P", 2 * APAIR), ("P", 2 * APAIR + 1)], writes=ebr)

                        def LNst(p):
                            kb, hp = steps[p]
                            c0, n, diag = geom(kb)
                            o = 2 * (p % 2)
                            eb, ebr = ebsel(p)
                            add("act", lambda e: e.activation(out=spb[:, o:o + 2, c0:c0 + n], in_=eb[:, :, c0:c0 + n], func=AF.Ln, bias=1.0),
                                reads=ebr, writes=[("spb", p % 2)])

                        def Bst(p):
                            kb, hp = steps[p]
                            ch = 2 * hg + hp
                            c0, n, diag = geom(kb)
                            step = b0 + 3 - kb
                            old = step % 2
                            o = 2 * (p % 2)
                            bp = BPAIR[p % 2]
                            for e_ in range(2):
                                ps_ = slice(64 * e_, 64 * e_ + 64)
                                b = 2 * bp + e_
                                mm(P[b][:, c0:c0 + n], ksb[ps_, ch, kb * 128:(kb + 1) * 128], qm[ps_, ch, c0:c0 + n], True, False,
                                   reads=[("ksb", ch, kb // 4), ("qm", ch)], writes=[("P", b)])
                            for e_ in range(2):
                                b = 2 * bp + e_
                                hh = 2 * hp + e_
                                if diag:
                                    mm(P[b][:, c0:c0 + 128], ident, sbmask, False, False, reads=["cb"], writes=[("P", b)])
                                mm(P[b][:, c0:c0 + n], negtri, spb[:, o + e_, c0:c0 + n], False, step == 0,
                                   reads=["cb", ("spb", p % 2)], writes=[("P", b)])
                                if step > 0:
                                    mm(P[b][:, c0:c0 + n], negones, sps(hh, old, c0, n), False, True,
                                       reads=["cb", ("spsum", hh, old)], writes=[("P", b)])
                            add("dve", lambda e: e.tensor_tensor(out=spair(hp, 1 - old, c0, n), in0=spair(hp, old, c0, n),
                                                                 in1=spb[:, o:o + 2, c0:c0 + n], op=ALU.add),
                                reads=[("spsum", 2 * hp, old), ("spsum", 2 * hp + 1, old), ("spb", p % 2)],
                                writes=[("spsum", 2 * hp, 1 - old), ("spsum", 2 * hp + 1, 1 - old)])

                        def EBst(p):
                            kb, hp = steps[p]
                            c0, n, diag = geom(kb)
                            o = 2 * (p % 2)
                            bp = BPAIR[p % 2]
                            add("act", lambda e: e.activation(out=wb[:, o:o + 2, c0:c0 + n], in_=PPv(bp)[:, :, c0:c0 + n], func=AF.Exp),
                                reads=[("P", 2 * bp), ("P", 2 * bp + 1)], writes=[("wb", p % 2)])

                        def PVst(p):
                            kb, hp = steps[p]
                            c0, n, diag = geom(kb)
                            o = 2 * (p % 2)
                            cbk = CB[hp]
                            for e_ in range(2):
                                hh = 2 * hp + e_
                                h = 4 * hg + hh
                                ps_ = slice(64 * e_, 64 * e_ + 64)
                                fs = firstC[hh]
                                firstC[hh] = False
                                mm(P[cbk][ps_, c0:c0 + n], vsb[:, kb, 64 * h:64 * h + 64], wb[:, o + e_, c0:c0 + n], fs, kb == 0,
                                   reads=[("vsb", kb), ("wb", p % 2)], writes=[("P", cbk)], skip_group_check=True)

                        Zst(0)
                        EAst(0)
                        LNst(0)
                        if npair > 1:
                            Zst(1)
                        for p in range(npair):
                            if p + 1 < npair:
                                EAst(p + 1)
                            fill(NFILL)
                            if p + 2 < npair:
                                Zst(p + 2)
                            fill(NFILL2)
                            if p >= 1:
                                EBst(p - 1)
                            Bst(p)
                            if p % 2 == 1:
                                trickle_cast()
                            if p >= 1:
                                PVst(p - 1)
                            if p + 1 < npair:
                                LNst(p + 1)
                        EBst(npair - 1)
                        PVst(npair - 1)
                        for cc in range(2):
                            c = 2 * hg + cc
                            add("dve", lambda e, cc=cc, c=c, CB=CB: e.tensor_copy(out=osb[:, c, :], in_=P[CB[cc]][:, :]),
                                reads=[("P", CB[cc])], writes=[("osb", c)])

                if filler is not None:
                    S_.tag = 'mlp'
                    for _ in filler:
                        pass
                S_.tag = 'merge'
                if LVL >= 5:
                    for pc in range(4):
                        sA = begin_slab(10 + 3 * pc)
                        for cc in range(2):
                            c = 2 * pc + cc
                            bk = [0, 1, 2, 3] if c % 2 == 0 else [4, 5, 6, 7]
                            proj_fm(sA, (2 * cc) * 512, 4, osb, "osb", bk[0])
                            proj_fm(sA, (2 * cc + 1) * 512, 4, osw, "osw", bk[1])
                            sGc = begin_slab(10 + 3 * pc + 1 + cc, live=(1 if cc == 0 else 0))
                            proj_fm(sGc, 0, KC, hT, "hT", bk[2])
                            proj_fm(sGc, 1024, KC, hT, "hT", bk[3])
                            sgo = 2 * (c % 2)
                            add("act", lambda e, bk=bk, sgo=sgo: e.activation(out=scr[:, sgo, :], in_=P[bk[2]][:, :], func=AF.Sigmoid),
                                reads=[("P", bk[2])], writes=[("scr", sgo)])
                            add("act", lambda e, bk=bk, sgo=sgo: e.activation(out=scr[:, sgo + 1, :], in_=P[bk[3]][:, :], func=AF.Sigmoid),
                                reads=[("P", bk[3])], writes=[("scr", sgo + 1)])
                            rs = c % 2
                            add("dve", lambda e, bk=bk, sgo=sgo, rs=rs: e.tensor_tensor(out=scr[:, 4, :], in0=P[bk[0]][:, :], in1=scr[:, sgo, :], op=ALU.mult),
                                reads=[("P", bk[0]), ("scr", sgo)], writes=[("scr", 4)])
                            add("dve", lambda e, bk=bk, sgo=sgo, rs=rs: e.tensor_tensor(out=scr[:, 5, :], in0=P[bk[1]][:, :], in1=scr[:, sgo + 1, :], op=ALU.mult),
                                reads=[("P", bk[1]), ("scr", sgo + 1)], writes=[("scr", 5)])
                            add("dve", lambda e, rs=rs, c=c: e.tensor_tensor(out=qm[:, c, :], in0=scr[:, 4, :], in1=scr[:, 5, :], op=ALU.add),
                                reads=[("scr", 4), ("scr", 5)], writes=[("qm", c)])
                if gi + 1 < NTL:
                    emit_load(gi + 1, eng="sp")
                gn = gi + 1
                if gn < NTL:
                    S_.tag = 'n1'
                    xtn = xt2[:, (gn % 2) * KC:(gn % 2 + 1) * KC, :]
                    norm_sq(xtn, gn % 2)
                S_.tag = 'wout'
                if LVL >= 6:
                    for c in range(KC):
                        slot = begin_slab(22 + c // 2)
                        bank = pbank([0, 1, 2, 3])
                        proj_fm(slot, (c % 2) * 1024, KC, qm, "qm", bank)
                        add("dve", lambda e, c=c, bank=bank, xt=xt: e.tensor_tensor(out=xt[:, c, :], in0=P[bank][:, :], in1=xt[:, c, :], op=ALU.add),
                            reads=[("P", bank), ("xt", xp, c)], writes=[("xt", xp, c)])
                if gn < NTL:
                    S_.tag = 'n1'
                    norm_rest(gn // NT, 0, xtn, gn % 2, hT, "hT")
        gl = NTL - 1
        S_.tag = 'mlp'
        rmsnorm(gl // NT, 8, xt2[:, (gl % 2) * KC:(gl % 2 + 1) * KC, :], gl % 2, h2T, "h2T")
        for _ in mlp_gen(gl):
            pass
        add("sp", None, extra_deps=[S_.dma_last["xst0"], S_.dma_last["xst1"]])

        S_.finalize()
        with nc.Block() as block:
            @block.tensor
            def _(e):
                S_.replay("pe", e, esem, dsem)

            @block.scalar
            def _(e):
                S_.replay("act", e, esem, dsem)

            @block.vector
            def _(e):
                S_.replay("dve", e, esem, dsem)

            @block.gpsimd
            def _(e):
                S_.replay("pool", e, esem, dsem)

            @block.sync
            def _(e):
                S_.replay("sp", e, esem, dsem)
    nc._n_ops = len(S_.ops)
    nc._sched = S_
    return nc


def _fm_job(W, cols, kcn, k0=0):
    blk = W[k0 * 128:(k0 + kcn) * 128][:, cols].reshape(kcn, 128, len(cols))
    return np.ascontiguousarray(blk.transpose(1, 0, 2)).reshape(128, kcn * len(cols))


def pack_layer(w_in, w_bsb, w_bsw, w_out, w_up, w_down):
    slabs = np.zeros((NSLAB, 128, SLAB), np.float32)
    ar = np.arange

    def put(s, off, blk):
        slabs[s, :, off:off + blk.shape[1]] = blk

    for c in range(4):
        put(c // 2, (c % 2) * 1024, _fm_job(w_in, ar(c * 128, c * 128 + 128), 8))
        put(2 + c // 2, (c % 2) * 1024, _fm_job(w_in, 512 + ar(c * 128, c * 128 + 128), 8))
        put(4 + c // 2, (c % 2) * 1024, _fm_job(w_in, 1536 + ar(c * 128, c * 128 + 128), 8))
    for half in range(2):
        put(7 + half, 0, _fm_job(w_in, 1024 + ar(512), 4, k0=4 * half))
    for g in range(2):
        cols = 2048 + 64 * g + np.concatenate([ar(64), ar(64)])
        put(6, g * 1024, _fm_job(w_in, cols, 8))
    put(9, 0, _fm_job(w_in, 2176 + ar(128), 8))
    for pc in range(4):
        for cc in range(2):
            c = 2 * pc + cc
            put(10 + 3 * pc, (2 * cc) * 512, _fm_job(w_bsb, ar(c * 128, c * 128 + 128), 4))
            put(10 + 3 * pc, (2 * cc + 1) * 512, _fm_job(w_bsw, ar(c * 128, c * 128 + 128), 4))
            put(10 + 3 * pc + 1 + cc, 0, _fm_job(w_in, 2304 + ar(c * 128, c * 128 + 128), 8))
            put(10 + 3 * pc + 1 + cc, 1024, _fm_job(w_in, 2304 + 1024 + ar(c * 128, c * 128 + 128), 8))
    for c in range(8):
        put(22 + c // 2, (c % 2) * 1024, _fm_job(w_out, ar(c * 128, c * 128 + 128), 8))
    for hh in range(2):
        for j in range(16):
            jj = 16 * hh + j
            put(26 + 16 * hh + j // 2, (j % 2) * 1024, _fm_job(w_up, ar(jj * 128, jj * 128 + 128), 8))
        for c in range(8):
            put(26 + 16 * hh + 8 + c, 0, _fm_job(w_down, ar(c * 128, c * 128 + 128), 16, k0=16 * hh))
    return slabs


def make_consts():
    cst = np.zeros((128, NCST), np.float32)
    i = np.arange(128)
    cst[:, 0:128] = np.eye(128, dtype=np.float32)
    cst[:, 128:256] = -(i[:, None] >= i[None, :]).astype(np.float32)
    cst[:, 256:384] = -1.0
    cst[:, 384:512] = 1.0
    cst[:, 512:640] = ((i[:, None] // 64) == (i[None, :] // 64)).astype(np.float32)
    s_, t_ = i[:, None], i[None, :]
    cst[:, 640:768] = np.where(s_ < t_, 0.0, NEG)
    mC = np.where(s_ <= t_, 0.0, NEG)
    mP = np.where(s_ > t_, 0.0, NEG)
    m = np.concatenate([mC, mP], axis=1)
    cst[:, 768:1024] = m
    cst[:, 1024:1280] = m
    RT = np.zeros((128, 128), np.float32)
    for d in range(128):
        hd, dd = divmod(d, 64)
        if dd < 32:
            RT[hd * 64 + dd + 32, d] = -1.0
        else:
            RT[hd * 64 + dd - 32, d] = 1.0
    cst[:, NCB:NCB + 128] = RT
    return cst


def make_rope(S):
    inv_freq = (1.0 / (np.float32(10000.0) ** (np.arange(0, 64, 2, dtype=np.float32) / np.float32(64)))).astype(np.float32)
    ang = np.arange(S, dtype=np.float32)[:, None] * inv_freq[None, :]
    cos, sin = np.cos(ang).astype(np.float32), np.sin(ang).astype(np.float32)
    f = (np.arange(128) % 64) % 32
    return np.ascontiguousarray(np.stack([cos[:, f].T, sin[:, f].T], 0))


def make_gv(mix_norm_g, mlp_norm_g, q_norm_g, k_norm_g, sinks):
    depth = mix_norm_g.shape[0]
    gvv = np.zeros((128, depth * NGV), np.float32)
    p = np.arange(128)
    for l in range(depth):
        o = l * NGV
        gvv[:, o:o + 8] = mix_norm_g[l].reshape(8, 128).T
        gvv[:, o + 8:o + 16] = mlp_norm_g[l].reshape(8, 128).T
        gvv[:, o + 16] = q_norm_g[l][p % 64]
        gvv[:, o + 17] = k_norm_g[l][p % 64]
        for c in range(4):
            gvv[:, o + 18 + c] = sinks[l][2 * c + p // 64]
    return gvv


_CACHE = {}


def _get_nc(S, depth):
    key = (S, depth)
    if key not in _CACHE:
        _CACHE[key] = build(S, depth)
    return _CACHE[key]


def kernel(x, mix_norm_g, w_in, q_norm_g, k_norm_g, sinks, w_branch_sb, w_branch_swa, w_out, mlp_norm_g, w_up, w_down):
    x = np.asarray(x, np.float32)
    B, S, _ = x.shape
    depth = int(np.asarray(w_in).shape[0])
    f = lambda a: np.asarray(a, np.float32)
    wsl = np.concatenate([pack_layer(f(w_in[l]), f(w_branch_sb[l]), f(w_branch_swa[l]), f(w_out[l]), f(w_up[l]), f(w_down[l]))
                          for l in range(depth)], 0)
    gvv = make_gv(f(mix_norm_g), f(mlp_norm_g), f(q_norm_g), f(k_norm_g), f(sinks))
    cst = make_consts()
    rope = make_rope(S)
    nc = _get_nc(S, depth)
    in_maps = []
    for b in range(B):
        xT = np.ascontiguousarray(x[b].T).reshape(KC, 128, S)
        in_maps.append({"xT": xT, "wsl": wsl, "gv": gvv, "cst": cst, "rope": rope})
    res = run_bass_kernel_spmd(nc, in_maps, core_ids=list(range(B)))
    out = np.empty((B, S, D), np.float32)
    for b in range(B):
        out[b] = res.results[b]["yT"].reshape(D, S).T
    return out
```
